# Optimizing a Trainium2 kernel written in Bass

```python
import math
import jax
import jax.numpy as jnp
from jax import lax
import numpy as np

D_MODEL = 1024
BATCH = 4
SEQ = 4096
DEPTH = 4

GRID_W = 64
CTX_LEN = 256

MIXERS = ('pool', 'attn', 'ssm', 'gmlp')
N_MIXERS = len(MIXERS)
CTX_READING_MIXERS = ('attn', 'ssm')

DEEPNORM_ALPHA = (2.0 * DEPTH) ** 0.25
DEEPNORM_BETA = (8.0 * DEPTH) ** -0.25
LN_EPS = 1e-5
N_MODS = 6

POOL_WINDOWS = (2, 4, 8, 16)
N_POOL_GROUPS = len(POOL_WINDOWS)
POOL_GROUP = D_MODEL // N_POOL_GROUPS

HEAD_DIM = 64
N_Q_HEADS = D_MODEL // HEAD_DIM
N_KV_HEADS = N_Q_HEADS // 4
GQA_GROUP = N_Q_HEADS // N_KV_HEADS
Q_WIDTH = N_Q_HEADS * HEAD_DIM
KV_WIDTH = N_KV_HEADS * HEAD_DIM
WINDOW = 128
ATTN_BLOCK = 128
ROPE_BASE = 10000.0
NEG_INF = -1e30

SSM_GROUP = 16
SSM_N_GROUPS = D_MODEL // SSM_GROUP
SSM_STATE = 64
DT_MIN = 1e-3
DT_MAX = 1e-1

GMLP_CHUNK = 128
GMLP_HALF = 2 * D_MODEL
GMLP_HEADS = 8
GMLP_HEAD_DIM = GMLP_HALF // GMLP_HEADS

FFN_HIDDEN = 2816
CONV_WIDTH = 3

kernel_name = 'hybrid_interleaved_diffusion_trunk'


def layer_norm(x, g, b):
    xf = x.astype(jnp.float32)
    mu = jnp.mean(xf, axis=-1, keepdims=True)
    var = jnp.mean(jnp.square(xf - mu), axis=-1, keepdims=True)
    y = (xf - mu) * lax.rsqrt(var + LN_EPS) * g.astype(jnp.float32) + b.astype(jnp.float32)
    return y.astype(x.dtype)


def modulate(x, shift, scale):
    return x * (1.0 + scale) + shift


def ada_modulations(cond, w, b):
    return jnp.split(jax.nn.silu(cond) @ w + b, N_MODS, axis=-1)


def post_norm_residual(x, y, gate, g, b):
    return layer_norm(DEEPNORM_ALPHA * x + gate * y, g, b)


def pool_mixer(h, w, b, scale):
    bsz, n, _ = h.shape
    hf = h.astype(jnp.float32)
    csum = jnp.concatenate([jnp.zeros_like(hf[:, :1]), lax.cumsum(hf, axis=1)], axis=1)
    csum = csum.reshape(bsz, n + 1, N_POOL_GROUPS, POOL_GROUP)
    pos = jnp.arange(n)[:, None]
    win = jnp.array(POOL_WINDOWS)[None, :]
    lo = jnp.clip(pos - win // 2, 0, n)
    hi = jnp.clip(pos - win // 2 + win, 0, n)
    grp = jnp.arange(N_POOL_GROUPS)[None, :]
    mean = (csum[:, hi, grp] - csum[:, lo, grp]) / (hi - lo).astype(jnp.float32)[None, :, :, None]
    mixed = mean - hf.reshape(bsz, n, N_POOL_GROUPS, POOL_GROUP)
    y = jnp.einsum('bngc,gcd->bngd', mixed.astype(h.dtype), w) + b.reshape(N_POOL_GROUPS, POOL_GROUP)
    return y.reshape(bsz, n, D_MODEL) * scale


def rope_1d(x, pos):
    half = x.shape[-1] // 2
    freqs = ROPE_BASE ** (-jnp.arange(half, dtype=jnp.float32) / half)
    ang = pos.astype(jnp.float32)[:, None] * freqs[None, :]
    cos = jnp.cos(ang)[None, :, None, :]
    sin = jnp.sin(ang)[None, :, None, :]
    xf = x.astype(jnp.float32)
    x1, x2 = xf[..., :half], xf[..., half:]
    return jnp.concatenate([x1 * cos - x2 * sin, x1 * sin + x2 * cos], axis=-1).astype(x.dtype)


def axial_rope(x, row_pos, col_pos):
    half = HEAD_DIM // 2
    return jnp.concatenate([rope_1d(x[..., :half], row_pos), rope_1d(x[..., half:], col_pos)], axis=-1)


def banded_attention(q, k, v, k_ctx, v_ctx, sink):
    bsz, s_len = q.shape[:2]
    nb = s_len // ATTN_BLOCK
    scale = HEAD_DIM ** -0.5
    qb = q.reshape(bsz, nb, ATTN_BLOCK, N_KV_HEADS, GQA_GROUP, HEAD_DIM)
    pad = ((0, 0), (ATTN_BLOCK, ATTN_BLOCK), (0, 0), (0, 0))

    def band(t):
        tp = jnp.pad(t, pad).reshape(bsz, nb + 2, ATTN_BLOCK, N_KV_HEADS, HEAD_DIM)
        return jnp.concatenate([tp[:, :-2], tp[:, 1:-1], tp[:, 2:]], axis=2)

    kb, vb = band(k), band(v)
    s_loc = jnp.einsum('bnqhgd,bnkhd->bnhgqk', qb, kb).astype(jnp.float32) * scale
    q_pos = jnp.arange(nb)[:, None] * ATTN_BLOCK + jnp.arange(ATTN_BLOCK)[None, :]
    k_pos = (jnp.arange(nb)[:, None] - 1) * ATTN_BLOCK + jnp.arange(3 * ATTN_BLOCK)[None, :]
    rel = k_pos[:, None, :] - q_pos[:, :, None]
    valid = (jnp.abs(rel) <= WINDOW) & (k_pos[:, None, :] >= 0) & (k_pos[:, None, :] < s_len)
    s_loc = jnp.where(valid[None, :, None, None], s_loc, NEG_INF)
    s_ctx = jnp.einsum('bnqhgd,bkhd->bnhgqk', qb, k_ctx).astype(jnp.float32) * scale
    s_sink = jnp.broadcast_to(sink[None, None, :, :, None, None], s_loc.shape[:-1] + (1,))
    p = jax.nn.softmax(jnp.concatenate([s_loc, s_ctx, s_sink], axis=-1), axis=-1).astype(v.dtype)
    n_loc = 3 * ATTN_BLOCK
    n_ctx = k_ctx.shape[1]
    o = (jnp.einsum('bnhgqk,bnkhd->bnqhgd', p[..., :n_loc], vb)
         + jnp.einsum('bnhgqk,bkhd->bnqhgd', p[..., n_loc:n_loc + n_ctx], v_ctx))
    return o.reshape(bsz, s_len, Q_WIDTH)


def context_attention(q, k, v, sink):
    s = jnp.einsum('bqhgd,bkhd->bhgqk', q, k).astype(jnp.float32) * HEAD_DIM ** -0.5
    s_sink = jnp.broadcast_to(sink[None, :, :, None, None], s.shape[:-1] + (1,))
    p = jax.nn.softmax(jnp.concatenate([s, s_sink], axis=-1), axis=-1).astype(v.dtype)
    o = jnp.einsum('bhgqk,bkhd->bqhgd', p[..., :-1], v)
    return o.reshape(q.shape[0], q.shape[1], Q_WIDTH)


def attn_mixer(h_lat, h_ctx, w_qkv, w_o, sink, row_pos, col_pos, need_ctx_out):
    bsz, s_len, _ = h_lat.shape
    n_ctx = h_ctx.shape[1]
    qkv = h_lat @ w_qkv
    q = axial_rope(qkv[..., :Q_WIDTH].reshape(bsz, s_len, N_Q_HEADS, HEAD_DIM), row_pos, col_pos)
    k = axial_rope(qkv[..., Q_WIDTH:Q_WIDTH + KV_WIDTH].reshape(bsz, s_len, N_KV_HEADS, HEAD_DIM), row_pos, col_pos)
    v = qkv[..., Q_WIDTH + KV_WIDTH:].reshape(bsz, s_len, N_KV_HEADS, HEAD_DIM)
    kv_ctx = h_ctx @ w_qkv[:, Q_WIDTH:]
    k_ctx = kv_ctx[..., :KV_WIDTH].reshape(bsz, n_ctx, N_KV_HEADS, HEAD_DIM)
    v_ctx = kv_ctx[..., KV_WIDTH:].reshape(bsz, n_ctx, N_KV_HEADS, HEAD_DIM)
    sink_logit = sink.astype(jnp.float32).reshape(N_KV_HEADS, GQA_GROUP)
    q = q.reshape(bsz, s_len, N_KV_HEADS, GQA_GROUP, HEAD_DIM)
    y_lat = banded_attention(q, k, v, k_ctx, v_ctx, sink_logit) @ w_o
    y_ctx = None
    if need_ctx_out:
        q_ctx = (h_ctx @ w_qkv[:, :Q_WIDTH]).reshape(bsz, n_ctx, N_KV_HEADS, GQA_GROUP, HEAD_DIM)
        y_ctx = context_attention(q_ctx, k_ctx, v_ctx, sink_logit) @ w_o
    return y_lat, y_ctx


def s5_discretise(lam_re, lam_im, log_dt, b_re, b_im):
    lam = lax.complex(lam_re.astype(jnp.float32), lam_im.astype(jnp.float32))
    dt = jnp.exp(log_dt.astype(jnp.float32))[:, None]
    lam_bar = jnp.exp(lam * dt)
    b = lax.complex(b_re.astype(jnp.float32), b_im.astype(jnp.float32))
    b_bar = ((lam_bar - 1.0) / lam)[..., None] * b
    return lam_bar, b_bar


def _linear_recurrence(e1, e2):
    a1, b1 = e1
    a2, b2 = e2
    return a1 * a2, a2 * b1 + b2


def s5_scan(u, lam_bar, b_bar, s0, reverse):
    bu = jnp.einsum('bngc,gpc->bngp', u.astype(jnp.complex64), b_bar)
    if s0 is not None:
        edge = u.shape[1] - 1 if reverse else 0
        bu = bu.at[:, edge].add(lam_bar[None] * s0)
    a = jnp.broadcast_to(lam_bar, (1, u.shape[1]) + lam_bar.shape)
    _, states = lax.associative_scan(_linear_recurrence, (a, bu), reverse=reverse, axis=1)
    return states


def ssm_mixer(h_lat, h_ctx, lam_re, lam_im, log_dt, b_re, b_im, c_re, c_im, d_skip, w_a, w_b, need_ctx_out):
    def groups(h):
        return h.astype(jnp.float32).reshape(h.shape[0], h.shape[1], SSM_N_GROUPS, SSM_GROUP)

    def readout(states, c_mat):
        y = jnp.real(jnp.einsum('bngp,gcp->bngc', states, c_mat))
        return y.reshape(states.shape[0], states.shape[1], D_MODEL)

    def glu(y, dtype):
        g = jax.nn.gelu(y).astype(dtype)
        return (g @ w_a) * jax.nn.sigmoid(g @ w_b)

    u_lat, u_ctx = groups(h_lat), groups(h_ctx)
    d32 = d_skip.astype(jnp.float32)
    y_lat = d32 * h_lat.astype(jnp.float32)
    y_ctx = d32 * h_ctx.astype(jnp.float32) if need_ctx_out else None
    for direction, reverse in enumerate((False, True)):
        lam_bar, b_bar = s5_discretise(lam_re[direction], lam_im[direction], log_dt[direction],
                                       b_re[direction], b_im[direction])
        c_mat = lax.complex(c_re[direction].astype(jnp.float32), c_im[direction].astype(jnp.float32))
        ctx_states = s5_scan(u_ctx, lam_bar, b_bar, None, reverse)
        ctx_final = ctx_states[:, 0] if reverse else ctx_states[:, -1]
        y_lat = y_lat + readout(s5_scan(u_lat, lam_bar, b_bar, ctx_final, reverse), c_mat)
        if need_ctx_out:
            y_ctx = y_ctx + readout(ctx_states, c_mat)
    out_ctx = glu(y_ctx, h_ctx.dtype) if need_ctx_out else None
    return glu(y_lat, h_lat.dtype), out_ctx


def gmlp_mixer(h, w_in, b_in, ln_g, ln_b, w_s, b_s, w_out):
    bsz, n, _ = h.shape
    z = jax.nn.gelu(h @ w_in + b_in)
    u = z[..., :GMLP_HALF]
    v = layer_norm(z[..., GMLP_HALF:], ln_g, ln_b)
    vc = v.reshape(bsz, n // GMLP_CHUNK, GMLP_CHUNK, GMLP_HEADS, GMLP_HEAD_DIM)
    gate = jnp.einsum('hpq,bnqhc->bnphc', w_s, vc) + b_s.T[None, None, :, :, None]
    return (u * gate.reshape(bsz, n, GMLP_HALF)) @ w_out


def conv_ffn(h, w_up, conv_w, conv_b, w_down):
    n = h.shape[1]
    a = h @ w_up
    pad = CONV_WIDTH // 2
    ap = jnp.pad(a, ((0, 0), (pad, pad), (0, 0)))
    a = conv_b + ap[:, 0:n] * conv_w[0]
    for tap in range(1, CONV_WIDTH):
        a = a + ap[:, tap:tap + n] * conv_w[tap]
    val, gate = a[..., :FFN_HIDDEN], a[..., FFN_HIDDEN:]
    return (val * jax.nn.silu(gate)) @ w_down


def _normal(key, shape, std):
    return std * jax.random.normal(key, shape, dtype=jnp.float32)


def _n_layers_of(kind):
    return len(range(MIXERS.index(kind), DEPTH, N_MIXERS))


def setup_inputs(seed: int = 0) -> dict:
    key = jax.random.key(seed)
    keys = iter(jax.random.split(key, 48))
    D = D_MODEL
    n_pool, n_attn, n_ssm, n_gmlp = (_n_layers_of(k) for k in MIXERS)
    qkv_width = Q_WIDTH + 2 * KV_WIDTH
    G, P = SSM_N_GROUPS, SSM_STATE
    return {
        'x': _normal(next(keys), (BATCH, SEQ, D), 1.0),
        'c': _normal(next(keys), (BATCH, D), 1.0),
        'ctx': _normal(next(keys), (BATCH, CTX_LEN, D), 1.0),
        'c_ctx': _normal(next(keys), (D,), 1.0),
        'ada_w': _normal(next(keys), (DEPTH, D, N_MODS * D), 0.5 * D ** -0.5),
        'ada_b': _normal(next(keys), (DEPTH, N_MODS * D), 0.02),
        'ln1_g': 1.0 + _normal(next(keys), (DEPTH, D), 0.02),
        'ln1_b': _normal(next(keys), (DEPTH, D), 0.02),
        'ln2_g': 1.0 + _normal(next(keys), (DEPTH, D), 0.02),
        'ln2_b': _normal(next(keys), (DEPTH, D), 0.02),
        'ffn_w_up': _normal(next(keys), (DEPTH, D, 2 * FFN_HIDDEN), D ** -0.5),
        'ffn_conv_w': _normal(next(keys), (DEPTH, CONV_WIDTH, 2 * FFN_HIDDEN), CONV_WIDTH ** -0.5),
        'ffn_conv_b': _normal(next(keys), (DEPTH, 2 * FFN_HIDDEN), 0.02),
        'ffn_w_down': _normal(next(keys), (DEPTH, FFN_HIDDEN, D), FFN_HIDDEN ** -0.5 * DEEPNORM_BETA),
        'pool_w': _normal(next(keys), (n_pool, N_POOL_GROUPS, POOL_GROUP, POOL_GROUP), POOL_GROUP ** -0.5 * DEEPNORM_BETA),
        'pool_b': _normal(next(keys), (n_pool, D), 0.02),
        'pool_scale': 1.0 + _normal(next(keys), (n_pool, D), 0.02),
        'attn_w_qkv': _normal(next(keys), (n_attn, D, qkv_width), D ** -0.5),
        'attn_w_o': _normal(next(keys), (n_attn, Q_WIDTH, D), Q_WIDTH ** -0.5 * DEEPNORM_BETA),
        'attn_sink': _normal(next(keys), (n_attn, N_Q_HEADS), 0.5),
        'ssm_lambda_re': -0.5 + _normal(next(keys), (n_ssm, 2, G, P), 1e-3),
        'ssm_lambda_im': jnp.pi * jnp.arange(P, dtype=jnp.float32) + _normal(next(keys), (n_ssm, 2, G, P), 1e-3),
        'ssm_log_dt': jax.random.uniform(next(keys), (n_ssm, 2, G), dtype=jnp.float32,
                                         minval=math.log(DT_MIN), maxval=math.log(DT_MAX)),
        'ssm_b_re': _normal(next(keys), (n_ssm, 2, G, P, SSM_GROUP), (2 * SSM_GROUP) ** -0.5),
        'ssm_b_im': _normal(next(keys), (n_ssm, 2, G, P, SSM_GROUP), (2 * SSM_GROUP) ** -0.5),
        'ssm_c_re': _normal(next(keys), (n_ssm, 2, G, SSM_GROUP, P), (2 * P) ** -0.5),
        'ssm_c_im': _normal(next(keys), (n_ssm, 2, G, SSM_GROUP, P), (2 * P) ** -0.5),
        'ssm_d': _normal(next(keys), (n_ssm, D), 1.0),
        'ssm_w_glu_a': _normal(next(keys), (n_ssm, D, D), D ** -0.5 * DEEPNORM_BETA),
        'ssm_w_glu_b': _normal(next(keys), (n_ssm, D, D), D ** -0.5),
        'gmlp_w_in': _normal(next(keys), (n_gmlp, D, 2 * GMLP_HALF), D ** -0.5),
        'gmlp_b_in': _normal(next(keys), (n_gmlp, 2 * GMLP_HALF), 0.02),
        'gmlp_ln_g': 1.0 + _normal(next(keys), (n_gmlp, GMLP_HALF), 0.02),
        'gmlp_ln_b': _normal(next(keys), (n_gmlp, GMLP_HALF), 0.02),
        'gmlp_w_s': _normal(next(keys), (n_gmlp, GMLP_HEADS, GMLP_CHUNK, GMLP_CHUNK), GMLP_CHUNK ** -0.5),
        'gmlp_b_s': 1.0 + _normal(next(keys), (n_gmlp, GMLP_HEADS, GMLP_CHUNK), 0.02),
        'gmlp_w_out': _normal(next(keys), (n_gmlp, GMLP_HALF, D), GMLP_HALF ** -0.5 * DEEPNORM_BETA),
    }


def reference(x, c, ctx, c_ctx, ada_w, ada_b, ln1_g, ln1_b, ln2_g, ln2_b,
              ffn_w_up, ffn_conv_w, ffn_conv_b, ffn_w_down,
              pool_w, pool_b, pool_scale,
              attn_w_qkv, attn_w_o, attn_sink,
              ssm_lambda_re, ssm_lambda_im, ssm_log_dt, ssm_b_re, ssm_b_im, ssm_c_re, ssm_c_im,
              ssm_d, ssm_w_glu_a, ssm_w_glu_b,
              gmlp_w_in, gmlp_b_in, gmlp_ln_g, gmlp_ln_b, gmlp_w_s, gmlp_b_s, gmlp_w_out):
    seq_len = x.shape[1]
    ROWS = seq_len // GRID_W
    row_pos = jnp.repeat(jnp.arange(ROWS), GRID_W)
    col_pos = jnp.tile(jnp.arange(GRID_W), ROWS)
    x_lat, x_ctx = x, ctx
    for layer in range(DEPTH):
        kind = MIXERS[layer % N_MIXERS]
        j = layer // N_MIXERS
        ctx_out = any(MIXERS[m % N_MIXERS] in CTX_READING_MIXERS for m in range(layer + 1, DEPTH))
        ctx_in = ctx_out or kind in CTX_READING_MIXERS
        sh1, sc1, gt1, sh2, sc2, gt2 = [m[:, None, :] for m in ada_modulations(c, ada_w[layer], ada_b[layer])]
        h_lat = modulate(x_lat, sh1, sc1)
        h_ctx = None
        if ctx_in:
            csh1, csc1, cgt1, csh2, csc2, cgt2 = ada_modulations(c_ctx, ada_w[layer], ada_b[layer])
            h_ctx = modulate(x_ctx, csh1, csc1)
        if kind == 'pool':
            y_lat = pool_mixer(h_lat, pool_w[j], pool_b[j], pool_scale[j])
            y_ctx = pool_mixer(h_ctx, pool_w[j], pool_b[j], pool_scale[j]) if ctx_out else None
        elif kind == 'attn':
            y_lat, y_ctx = attn_mixer(h_lat, h_ctx, attn_w_qkv[j], attn_w_o[j], attn_sink[j],
                                      row_pos, col_pos, ctx_out)
        elif kind == 'ssm':
            y_lat, y_ctx = ssm_mixer(h_lat, h_ctx, ssm_lambda_re[j], ssm_lambda_im[j], ssm_log_dt[j],
                                     ssm_b_re[j], ssm_b_im[j], ssm_c_re[j], ssm_c_im[j], ssm_d[j],
                                     ssm_w_glu_a[j], ssm_w_glu_b[j], ctx_out)
        else:
            y_lat = gmlp_mixer(h_lat, gmlp_w_in[j], gmlp_b_in[j], gmlp_ln_g[j], gmlp_ln_b[j],
                               gmlp_w_s[j], gmlp_b_s[j], gmlp_w_out[j])
            y_ctx = (gmlp_mixer(h_ctx, gmlp_w_in[j], gmlp_b_in[j], gmlp_ln_g[j], gmlp_ln_b[j],
                                gmlp_w_s[j], gmlp_b_s[j], gmlp_w_out[j]) if ctx_out else None)
        x_lat = post_norm_residual(x_lat, y_lat, gt1, ln1_g[layer], ln1_b[layer])
        f_lat = conv_ffn(modulate(x_lat, sh2, sc2), ffn_w_up[layer], ffn_conv_w[layer],
                         ffn_conv_b[layer], ffn_w_down[layer])
        x_lat = post_norm_residual(x_lat, f_lat, gt2, ln2_g[layer], ln2_b[layer])
        if ctx_out:
            x_ctx = post_norm_residual(x_ctx, y_ctx, cgt1, ln1_g[layer], ln1_b[layer])
            f_ctx = conv_ffn(modulate(x_ctx, csh2, csc2), ffn_w_up[layer], ffn_conv_w[layer],
                             ffn_conv_b[layer], ffn_w_down[layer])
            x_ctx = post_norm_residual(x_ctx, f_ctx, cgt2, ln2_g[layer], ln2_b[layer])
    return x_lat
```

```python
import numpy as np
import concourse.bass as bass
import concourse.mybir as mybir
from concourse.bass_utils import run_bass_kernel_spmd
from contextlib import ExitStack

F32 = mybir.dt.float32
BF16 = mybir.dt.bfloat16
AF = mybir.ActivationFunctionType
ALU = mybir.AluOpType

D = 1024
NCH = 8
SEQ = 4096
HALF = 2048
CTX = 256
CHALF = 128
DEPTH = 4
ALPHA = (2.0 * DEPTH) ** 0.25
LN_EPS = 1e-5
FH = 2816
NFC = 22
POOL_WINDOWS = (2, 4, 8, 16)

EPOCH = 20000
DMA_RING = 6
SAME_ENG_SYNC = True


class _Op:
    __slots__ = ("eng", "fn", "deps", "sig", "dma", "dslot", "dround", "idx", "has_dep")

    def __init__(self, eng, fn, dma, idx):
        self.eng = eng
        self.fn = fn
        self.deps = []
        self.sig = None
        self.dma = dma
        self.dslot = None
        self.dround = None
        self.idx = idx
        self.has_dep = False


class Sched:
    def __init__(self):
        self.ops = []
        self.res = {}
        self.dma_count = {}

    def add(self, eng, fn, reads=(), writes=(), dma=False):
        ops = self.ops
        idx = len(ops)
        op = _Op(eng, fn, dma, idx)
        deps = set()
        for (name, lo, hi) in reads:
            lst = self.res.get(name, ())
            keep = []
            for r in lst:
                rlo, rhi, ri, rw = r
                if ri == idx:
                    keep.append(r)
                    continue
                if rw:
                    if rlo < hi and lo < rhi:
                        deps.add(ri)
                    keep.append(r)
                else:
                    o = ops[ri]
                    if (not dma) and (not o.dma) and o.eng == eng and lo <= rlo and rhi <= hi:
                        continue
                    keep.append(r)
            keep.append((lo, hi, idx, False))
            self.res[name] = keep
        for (name, lo, hi) in writes:
            lst = self.res.get(name, ())
            keep = []
            for r in lst:
                rlo, rhi, ri, rw = r
                if ri == idx:
                    keep.append(r)
                    continue
                if rlo < hi and lo < rhi:
                    deps.add(ri)
                    if lo <= rlo and rhi <= hi:
                        continue
                keep.append(r)
            keep.append((lo, hi, idx, True))
            self.res[name] = keep
        deps.discard(idx)
        op.deps = sorted(deps)
        if dma:
            n = self.dma_count.get(eng, 0)
            self.dma_count[eng] = n + 1
            op.dslot = n % DMA_RING
            op.dround = n // DMA_RING
        ops.append(op)
        return op

    def emit(self, nc, final_wait_ops=()):
        ops = self.ops
        for op in ops:
            for d in op.deps:
                o = ops[d]
                if o.dma:
                    continue
                if o.eng == op.eng and (not op.dma) and (o.eng == "pe" or not SAME_ENG_SYNC):
                    continue
                o.has_dep = True
        import os
        pe_ops = [op for op in ops if op.eng == "pe"]
        self.extra = {}
        PEK = int(os.environ.get("DBG_PEK", "16"))
        for i, op in enumerate(pe_ops):
            if i >= PEK and i % PEK == 0:
                o = pe_ops[i - PEK]
                o.has_dep = True
                self.extra[op.idx] = o
        if os.environ.get("DBG_SIGALL"):
            for op in ops:
                if op.eng == "pe":
                    op.has_dep = True
        counts = {}
        for op in ops:
            if op.has_dep and not op.dma:
                counts[op.eng] = counts.get(op.eng, 0) + 1
                op.sig = counts[op.eng]
        import os
        if os.environ.get("DBG_PRINT"):
            n0 = len(ops) - int(os.environ["DBG_PRINT"])
            for op in ops[n0:]:
                ds = []
                for d in op.deps:
                    o = ops[d]
                    ds.append("%d:%s%s" % (d, o.eng, ("[dma s%d r%d]" % (o.dslot, o.dround)) if o.dma else ("#%s" % o.sig)))
                print(op.idx, op.eng, "DMA s%d r%d" % (op.dslot, op.dround) if op.dma else "sig=%s" % op.sig, " <- ", " ".join(ds))
        sems = {}
        es = ExitStack()

        def getsem(key):
            if key not in sems:
                sems[key] = es.enter_context(nc.semaphore("s_%s" % "_".join(str(k) for k in key)))
            return sems[key]

        engs = {}
        for op in ops:
            engs.setdefault(op.eng, []).append(op)
        for e, lst in engs.items():
            for op in lst:
                if op.dma:
                    getsem(("d", e, op.dslot))
                elif op.sig is not None:
                    getsem(("c", e, (op.sig - 1) // EPOCH))

        def wait_for(engine_obj, seen, o):
            if o.dma:
                key = ("d", o.eng, o.dslot)
                val = 16 * (o.dround + 1)
            else:
                key = ("c", o.eng, (o.sig - 1) // EPOCH)
                val = (o.sig - 1) % EPOCH + 1
            if seen.get(key, 0) >= val:
                return
            seen[key] = val
            engine_obj.wait_ge(sems[key], val)

        def body(ename, engine_obj):
            lst = engs.get(ename, [])
            seen = {}
            for op in lst:
                for d in op.deps:
                    o = ops[d]
                    if not o.dma:
                        if o.sig is None:
                            continue
                        if o.eng == ename and (not op.dma) and (ename == "pe" or not SAME_ENG_SYNC):
                            continue
                    wait_for(engine_obj, seen, o)
                if op.idx in self.extra:
                    wait_for(engine_obj, seen, self.extra[op.idx])
                if op.dma:
                    if op.dround > 0:
                        key = ("d", ename, op.dslot)
                        val = 16 * op.dround
                        if seen.get(key, 0) < val:
                            seen[key] = val
                            engine_obj.wait_ge(sems[key], val)
                    ins = op.fn(engine_obj)
                    ins.then_inc(sems[("d", ename, op.dslot)], 16)
                else:
                    ins = op.fn(engine_obj)
                    if op.sig is not None:
                        ins.then_inc(sems[("c", ename, (op.sig - 1) // EPOCH)], 1)
            if ename == "sp":
                for o in final_wait_ops:
                    wait_for(engine_obj, seen, o)

        with nc.Block() as block:
            @block.tensor
            def _(e):
                body("pe", e)

            @block.scalar
            def _(e):
                body("act", e)

            @block.vector
            def _(e):
                body("dve", e)

            @block.gpsimd
            def _(e):
                body("pool", e)

            @block.sync
            def _(e):
                body("sp", e)
        es.close()


BIG = 1 << 30
ARENA_WORDS = 51000


class Tl:
    def __init__(self, h, name, ap=None, base=0, unit=1, n1=BIG):
        self.h = h
        self.name = name
        self.ap = ap if ap is not None else h
        self.base = base
        self.unit = unit
        self.n1 = n1

    def __getitem__(self, k):
        return self.ap[k]

    def r(self, lo=0, hi=BIG):
        hi = min(hi, self.n1)
        return (self.name, self.base + lo * self.unit, self.base + hi * self.unit)


class Rot:
    def __init__(self, tiles):
        self.tiles = tiles
        self.i = 0

    def next(self):
        t = self.tiles[self.i % len(self.tiles)]
        self.i += 1
        return t


class Builder:
    def __init__(self):
        self.nc = bass.Bass("TRN2", target_bir_lowering=False)
        self.S = Sched()
        self.es = ExitStack()
        self.sb_bytes = 0
        self.arena = None
        self.top = 0
        self.peak = 0
        self.outs = []
        self.uid = 0

    def sb(self, name, shape, dt):
        if self.arena is None:
            self.arena = self.es.enter_context(self.nc.sbuf_tensor("arena", [128, ARENA_WORDS], F32))
            self.top = 0
        n = 1
        for s_ in shape[1:]:
            n *= s_
        esz = 2 if dt == BF16 else 4
        words = ((n * esz + 63) // 64) * 16
        off = self.top
        self.top += words
        self.peak = max(self.peak, self.top)
        assert self.top <= ARENA_WORDS, ("SBUF arena overflow", name, self.top)
        v = self.arena[0:shape[0], off:off + words]
        if dt != F32:
            v = v.bitcast(dt)[:, 0:n]
        else:
            v = v[:, 0:n]
        if len(shape) == 3:
            v = v.rearrange("p (a b) -> p a b", a=shape[1])
        elif len(shape) == 4:
            v = v.rearrange("p (a b c) -> p a b c", a=shape[1], b=shape[2])
        unit = (n // shape[1]) * esz
        return Tl(None, "sb", v, base=off * 4, unit=unit, n1=shape[1])

    def mark(self):
        return self.top

    def release(self, m):
        self.top = m

    def rot(self, name, n, shape, dt):
        return Rot([self.sb("%s%d" % (name, i), shape, dt) for i in range(n)])

    def ps(self, name, shape, dt=F32):
        h = self.es.enter_context(self.nc.psum_tensor(name, list(shape), dt))
        return Tl(h, name)

    def din(self, name, shape, dt=F32):
        return Tl(None, name, self.nc.dram_tensor(name, list(shape), dt, kind="ExternalInput").ap())

    def dout(self, name, shape, dt=F32):
        return Tl(None, name, self.nc.dram_tensor(name, list(shape), dt, kind="ExternalOutput").ap())

    def dscr(self, name, shape, dt=F32):
        return Tl(None, name, self.nc.dram_tensor(name, list(shape), dt).ap())

    def dma(self, q, out, in_, reads=(), writes=(), **kw):
        return self.S.add(q, lambda e: e.dma_start(out=out, in_=in_, **kw), reads, writes, dma=True)

    def act(self, out, in_, func, reads, writes, scale=None, bias=None, eng="act"):
        kw = {}
        if scale is not None:
            kw["scale"] = scale
        if bias is not None:
            kw["bias"] = bias
        return self.S.add(eng, lambda e: e.activation(out=out, in_=in_, func=func, **kw), reads, writes)

    def mm(self, out, lhsT, rhs, start, stop, reads, writes):
        return self.S.add("pe", lambda e: e.matmul(out, lhsT=lhsT, rhs=rhs, start=start, stop=stop), reads, writes)

    def tt(self, out, in0, in1, op, reads, writes, eng="dve"):
        return self.S.add(eng, lambda e: e.tensor_tensor(out=out, in0=in0, in1=in1, op=op), reads, writes)

    def stt(self, out, in0, scalar, in1, op0, op1, reads, writes, eng="dve"):
        return self.S.add(eng, lambda e: e.scalar_tensor_tensor(out=out, in0=in0, scalar=scalar, in1=in1,
                                                                op0=op0, op1=op1), reads, writes)

    def ts(self, out, in0, s1, s2, op0, op1, reads, writes, eng="dve"):
        if op1 is None:
            return self.S.add(eng, lambda e: e.tensor_scalar(out=out, in0=in0, scalar1=s1, scalar2=None, op0=op0),
                              reads, writes)
        return self.S.add(eng, lambda e: e.tensor_scalar(out=out, in0=in0, scalar1=s1, scalar2=s2, op0=op0, op1=op1),
                          reads, writes)

    def memset(self, ap, val, writes, eng="pool"):
        return self.S.add(eng, lambda e: e.memset(ap, val), (), writes)

    def copy(self, out, in_, reads, writes, eng="dve"):
        return self.S.add(eng, lambda e: e.tensor_copy(out=out, in_=in_), reads, writes)

    def finish(self):
        self.S.emit(self.nc, final_wait_ops=self.outs)
        self.es.close()
        return self.nc


def tiles_of(n, tmax):
    k = (n + tmax - 1) // tmax
    base = n // k
    rem = n - base * k
    out = []
    t0 = 0
    for i in range(k):
        t = base + (1 if i < rem else 0)
        out.append((t0, t))
        t0 += t
    return out


class Common:
    def __init__(self, b: Builder, segs, ffn=True):
        self.b = b
        self.segs = segs
        self.ffn = ffn
        nc = b.nc
        self.cc_d = b.din("cc", [128, NCH, 2])
        self.adaw_d = b.din("ada_w", [128, NCH, 6 * D])
        self.adab_d = b.din("ada_b", [128, 48])
        self.lnp_d = b.din("lnp", [128, 4, NCH])
        if ffn:
            self.wup_d = b.din("w_up", [128, NCH, 2 * FH])
            self.wdn_d = b.din("w_dn", [128, NFC, D])
            self.convp_d = b.din("convp", [128, 2 * NFC, 4])
        self.ones = b.sb("ones", [128, 128], F32)
        b.memset(self.ones[:], 1.0, [self.ones.r()])
        self.zerov = b.sb("zerov", [128, 1], F32)
        b.memset(self.zerov[:], 0.0, [self.zerov.r()])
        self.epsv = b.sb("epsv", [128, 1], F32)
        b.memset(self.epsv[:], LN_EPS, [self.epsv.r()])
        self.pg = Rot([b.ps("pg%d" % i, [128, 512]) for i in range(4)])
        self.pab = [b.ps("pa%d" % i, [128, 512]) for i in range(4)]
        self.lnp = b.sb("lnp_s", [128, 4, NCH], F32)
        b.dma("sp", self.lnp[:], self.lnp_d[:], (), [self.lnp.r()])
        if ffn:
            self.convp = b.sb("convp_s", [128, 2 * NFC, 4], F32)
            b.dma("sp", self.convp[:], self.convp_d[:], (), [self.convp.r()])
        self.modv = b.sb("modv", [128, 48, 2], F32)
        self.der = b.sb("der", [128, 8, NCH, 2], F32)

    def alloc_ln(self, width):
        b = self.b
        self.stat = b.rot("stat", 1, [128, 4, width], F32)
        self.sq = b.rot("sq", 1, [128, NCH, width], F32)
        self.xn = b.rot("xn", 1, [128, NCH, width], F32)

    def mod(self, j, c, col):
        return self.modv[:, j * 8 + c, col:col + 1]

    def dv(self, k, c, col):
        return self.der[:, k, c, col:col + 1]

    def emit_mods(self):
        b = self.b
        cc = b.sb("cc_s", [128, NCH, 2], F32)
        b.dma("sp", cc[:], self.cc_d[:], (), [cc.r()])
        sc = b.sb("silu_c", [128, NCH, 2], F32)
        b.act(sc[:], cc[:], AF.Silu, [cc.r()], [sc.r()])
        adab = b.sb("adab_s", [128, 48], F32)
        b.dma("sp", adab[:], self.adab_d[:], (), [adab.r()])
        mk = b.mark()
        wb = b.rot("adaw", 2, [128, NCH, 512], F32)
        pm = self.pg.next()
        for blk in range(12):
            w = wb.next()
            b.dma("sp", w[:], self.adaw_d[:, :, blk * 512:(blk + 1) * 512], (), [w.r()])
            for j in range(4):
                m = blk * 4 + j
                for kc in range(NCH):
                    b.mm(pm[:, 2 * m:2 * m + 2], w[:, kc, j * 128:(j + 1) * 128], sc[:, kc, :],
                         kc == 0, kc == NCH - 1, [w.r(), sc.r()], [pm.r(2 * m, 2 * m + 2)])
        b.release(mk)
        b.tt(self.modv[:], pm[:, 0:96].rearrange("p (m t) -> p m t", t=2),
             adab[:].unsqueeze(2).broadcast_to([128, 48, 2]), ALU.add,
             [pm.r(), adab.r()], [self.modv.r()])
        mv = self.modv
        der = self.der
        W = [der.r()]
        R = [mv.r(), self.lnp.r(), der.r()]
        b.ts(der[:, 0], mv[:, 8:16, :], 1.0, None, ALU.add, None, R, W)
        b.ts(der[:, 1], mv[:, 32:40, :], 1.0, None, ALU.add, None, R, W)
        g1 = self.lnp[:, 0, :].unsqueeze(2).broadcast_to([128, NCH, 2])
        b1 = self.lnp[:, 1, :].unsqueeze(2).broadcast_to([128, NCH, 2])
        b.tt(der[:, 2], der[:, 1], g1, ALU.mult, R, W)
        b.tt(der[:, 3], der[:, 1], b1, ALU.mult, R, W)
        b.tt(der[:, 3], der[:, 3], mv[:, 24:32, :], ALU.add, R, W)

    def emit_ln(self, u, T, outs):
        b = self.b
        sq = self.sq.next()
        b.act(sq[:, :, 0:T], u[:, :, 0:T], AF.Square, [u.r()], [sq.r()])
        p1 = self.pg.next()
        p2 = self.pg.next()
        for c in range(NCH):
            b.mm(p1[:, 0:T], self.ones[:], u[:, c, 0:T], c == 0, c == NCH - 1, [self.ones.r(), u.r()], [p1.r()])
        for c in range(NCH):
            b.mm(p2[:, 0:T], self.ones[:], sq[:, c, 0:T], c == 0, c == NCH - 1, [self.ones.r(), sq.r()], [p2.r()])
        st = self.stat.next()
        mean, msq, rstd, nmr = st[:, 0, 0:T], st[:, 1, 0:T], st[:, 2, 0:T], st[:, 3, 0:T]
        RW = [st.r()]
        b.act(mean, p1[:, 0:T], AF.Copy, [p1.r()], RW, scale=1.0 / D)
        b.tt(msq, mean, mean, ALU.mult, RW, RW)
        b.stt(rstd, p2[:, 0:T], 1.0 / D, msq, ALU.mult, ALU.subtract, [p2.r(), st.r()], RW)
        b.act(rstd, rstd, AF.Sqrt, RW + [self.epsv.r()], RW, bias=self.epsv[:, 0:1])
        b.S.add("dve", lambda e, rstd=rstd: e.reciprocal(out=rstd, in_=rstd), RW, RW)
        b.stt(nmr, mean, -1.0, rstd, ALU.mult, ALU.mult, RW, RW)
        xn = self.xn.next()
        b.tt(xn[:, :, 0:T], u[:, :, 0:T], st[:, 2, 0:T].unsqueeze(1).broadcast_to([128, NCH, T]), ALU.mult,
             [u.r(), st.r()], [xn.r()])
        b.tt(xn[:, :, 0:T], xn[:, :, 0:T], st[:, 3, 0:T].unsqueeze(1).broadcast_to([128, NCH, T]), ALU.add,
             [xn.r(), st.r()], [xn.r()])
        for (dst_fn, Tw, sc_fn, bi_fn, wres) in outs:
            for c in range(NCH):
                b.act(dst_fn(c), xn[:, c, 0:Tw], AF.Identity, [xn.r(), self.der.r(), self.lnp.r()], [wres],
                      scale=sc_fn(c), bias=bi_fn(c))

    def ln1_store(self, u, name, n, col, t0, T, x1s_r):
        b = self.b
        h2 = self.h2[name]
        x1 = self.x1[name]
        To = min(T, n - t0)
        outs = [(lambda c: h2[:, c, t0 + 1:t0 + 1 + T], T,
                 lambda c: self.dv(2, c, col), lambda c: self.dv(3, c, col), h2.r())]
        if To > 0:
            x1s = x1s_r.next()
            outs.append((lambda c: x1s[:, c, 0:To], To,
                         lambda c: self.lnp[:, 0, c:c + 1], lambda c: self.lnp[:, 1, c:c + 1], x1s.r()))
        self.emit_ln(u, T, outs)
        if To > 0:
            b.dma("sp", x1[:, :, t0:t0 + To].rearrange("c p t -> p c t"), x1s[:, :, 0:To], [x1s.r()],
                  [x1.r(t0, t0 + To)])

    def alloc_ffn(self, h2cols=None):
        b = self.b
        self.h2 = {}
        self.x1 = {}
        tot = 0
        self.goff = {}
        for (name, n, col) in self.segs:
            self.h2[name] = b.sb("h2_" + name, [128, NCH, max(n + 2, (h2cols or {}).get(name, 0))], BF16)
            self.x1[name] = b.dscr("x1_" + name, [NCH, 128, n], F32)
            b.memset(self.h2[name][:, :, 0:1], 0.0, [self.h2[name].r()])
            self.goff[name] = tot
            tot += n
        self.gtot = tot
        self.gtiles = {}
        k = 0
        for (name, n, col) in self.segs:
            self.gtiles[name] = k
            k += (n + 255) // 256
        self.G = b.dscr("Gscr", [k, 128, NFC, 256], BF16)

    def emit_ffn(self, out_d, stage=9):
        b = self.b
        mk = b.mark()
        self.tmpa = b.rot("tmpa", 3, [128, 512], F32)
        self.tmpb = b.rot("tmpb", 3, [128, 512], F32)
        self.tmps = b.rot("tmps", 3, [128, 512], F32)
        wv_r = b.rot("wv", 3, [128, NCH, 128], BF16)
        wg_r = b.rot("wg", 3, [128, NCH, 128], BF16)
        gt_r = b.rot("gtile", 3, [128, 512], BF16)
        cp = self.convp
        for f in range(NFC):
            wv = wv_r.next()
            wg = wg_r.next()
            b.dma("pool", wv[:], self.wup_d[:, :, f * 128:(f + 1) * 128], (), [wv.r()])
            b.dma("pool", wg[:], self.wup_d[:, :, FH + f * 128:FH + (f + 1) * 128], (), [wg.r()])
            for (name, n, col) in self.segs:
                h2 = self.h2[name]
                for ti, (t0, T) in enumerate(tiles_of(n, 256)):
                    pv = self.pg.next()
                    pgt = self.pg.next()
                    for kc in range(NCH):
                        b.mm(pv[:, 0:T + 2], wv[:, kc, :], h2[:, kc, t0:t0 + T + 2], kc == 0, kc == NCH - 1,
                             [wv.r(), h2.r()], [pv.r()])
                    for kc in range(NCH):
                        b.mm(pgt[:, 0:T + 2], wg[:, kc, :], h2[:, kc, t0:t0 + T + 2], kc == 0, kc == NCH - 1,
                             [wg.r(), h2.r()], [pgt.r()])
                    ta = self.tmpa.next()
                    tb = self.tmpb.next()
                    tsl = self.tmps.next()
                    for (ps_, dst, ch) in ((pv, ta, f), (pgt, tb, NFC + f)):
                        b.act(dst[:, 0:T], ps_[:, 1:T + 1], AF.Identity, [ps_.r(), cp.r()], [dst.r()],
                              scale=cp[:, ch, 1:2], bias=cp[:, ch, 3:4])
                        b.stt(dst[:, 0:T], ps_[:, 0:T], cp[:, ch, 0:1], dst[:, 0:T], ALU.mult, ALU.add,
                              [ps_.r(), cp.r(), dst.r()], [dst.r()])
                        b.stt(dst[:, 0:T], ps_[:, 2:T + 2], cp[:, ch, 2:3], dst[:, 0:T], ALU.mult, ALU.add,
                              [ps_.r(), cp.r(), dst.r()], [dst.r()])
                    b.act(tsl[:, 0:T], tb[:, 0:T], AF.Silu, [tb.r()], [tsl.r()])
                    g = gt_r.next()
                    b.tt(g[:, 0:T], ta[:, 0:T], tsl[:, 0:T], ALU.mult, [ta.r(), tsl.r()], [g.r()], eng="pool")
                    gi = self.gtiles[name] + ti
                    b.dma("sp", self.G[gi, :, f, 0:T], g[:, 0:T], [g.r()], [self.G.r(gi * NFC + f, gi * NFC + f + 1)])
        b.release(mk)
        if stage == 2:
            dbg = b.dout("dbg", [9, 128, NFC, 256], BF16)
            b.outs.append(b.dma("sp", dbg[:], self.G[:], [self.G.r()], []))
            return
        self.alloc_ln(256)
        wd = b.sb("wd", [128, NFC, D], BF16)
        import os
        skip = os.environ.get("DBG_SKIP", "")
        for f in range(NFC):
            if "wd" in skip:
                break
            b.dma("pool", wd[:, f, :], self.wdn_d[:, f, :], (), [wd.r(f, f + 1)])
        gl_r = b.rot("gload", 2, [128, NFC, 256], BF16)
        u_r = b.rot("u2", 2, [128, NCH, 256], F32)
        x1t_r = b.rot("x1t", 1, [128, NCH, 256], F32)
        xo_r = b.rot("xo", 1, [128, NCH, 256], F32)
        gtmp_r = b.rot("gtmp", 2, [128, 256], F32)
        for (name, n, col) in self.segs:
            x1 = self.x1[name]
            for ti, (t0, T) in enumerate(tiles_of(n, 256)):
                gl = gl_r.next()
                gi = self.gtiles[name] + ti
                if "gload" not in skip:
                    b.dma("sp", gl[:], self.G[gi], [self.G.r(gi * NFC, (gi + 1) * NFC)], [gl.r()])
                u = u_r.next()
                x1t = x1t_r.next()
                if "x1l" not in skip:
                    b.dma("sp", x1t[:, :, 0:T], x1[:, :, t0:t0 + T].rearrange("c p t -> p c t"),
                          [x1.r(t0, t0 + T)], [x1t.r()])
                for bk in range(4):
                    pa = self.pab[bk]
                    for dc in (2 * bk, 2 * bk + 1):
                        po = (dc % 2) * 256
                        for f in range(NFC):
                            b.mm(pa[:, po:po + T], wd[:, f, dc * 128:(dc + 1) * 128], gl[:, f, 0:T],
                                 f == 0, f == NFC - 1, [wd.r(f, f + 1), gl.r()], [pa.r()])
                    for dc in (2 * bk, 2 * bk + 1):
                        po = (dc % 2) * 256
                        gtmp = gtmp_r.next()
                        b.act(gtmp[:, 0:T], pa[:, po:po + T], AF.Identity,
                              [pa.r(), self.modv.r()], [gtmp.r()], scale=self.mod(5, dc, col))
                        b.stt(u[:, dc, 0:T], x1t[:, dc, 0:T], ALPHA, gtmp[:, 0:T], ALU.mult, ALU.add,
                              [x1t.r(), gtmp.r()], [u.r()])
                xo = xo_r.next()
                if "ln" in skip:
                    od = out_d[name]
                    op = b.dma("sp", od[:, :, t0:t0 + T].rearrange("c p t -> p c t"), u[:, :, 0:T], [u.r()], [])
                    b.outs.append(op)
                    continue
                self.emit_ln(u, T, [(lambda c, xo=xo, T=T: xo[:, c, 0:T], T,
                                     lambda c: self.lnp[:, 2, c:c + 1],
                                     lambda c: self.lnp[:, 3, c:c + 1], xo.r())])
                od = out_d[name]
                op = b.dma("sp", od[:, :, t0:t0 + T].rearrange("c p t -> p c t"), xo[:, :, 0:T], [xo.r()], [])
                b.outs.append(op)


def build_layer0(stage=9):
    b = Builder()
    NL = HALF + 1
    NC_ = CHALF + 1
    segs = [("l", HALF, 0), ("c", CHALF, 1)]
    cm = Common(b, segs)
    PADL = 8
    xin = {"l": b.din("xin_l", [NCH, 128, PADL + NL + 15]), "c": b.din("xin_c", [NCH, 128, PADL + NC_ + 15])}
    out_d = {"l": b.dout("out_l", [NCH, 128, HALF]), "c": b.dout("out_c", [NCH, 128, CHALF])}
    poolw_d = b.din("pool_w", [128, 4, 2, 256])
    poolp_d = b.din("pool_p", [128, 2, NCH])
    flags_d = b.din("flags", [128, 2])
    corr_d = b.din("corr", [128, NCH, 8])
    cm.emit_mods()
    if stage == 0:
        dbg = b.dout("dbg", [128, 96])
        b.outs.append(b.dma("sp", dbg[:], cm.modv[:].rearrange("p m t -> p (m t)"), [cm.modv.r()], []))
        return b.finish()
    cm.alloc_ffn()
    poolw = b.sb("poolw", [128, 4, 2, 256], BF16)
    b.dma("pool", poolw[:], poolw_d[:], (), [poolw.r()])
    poolp = b.sb("poolp", [128, 2, NCH], F32)
    b.dma("sp", poolp[:], poolp_d[:], (), [poolp.r()])
    flags = b.sb("flags_s", [128, 2], F32)
    b.dma("sp", flags[:], flags_d[:], (), [flags.r()])
    corr = b.sb("corr_s", [128, NCH, 8], F32)
    b.dma("sp", corr[:], corr_d[:], (), [corr.r()])
    der = cm.der
    R = [cm.modv.r(), poolp.r(), der.r()]
    W = [der.r()]
    b.tt(der[:, 4], cm.modv[:, 16:24, :], poolp[:, 1, :].unsqueeze(2).broadcast_to([128, NCH, 2]), ALU.mult, R, W)
    b.tt(der[:, 5], der[:, 4], poolp[:, 0, :].unsqueeze(2).broadcast_to([128, NCH, 2]), ALU.mult, R, W)

    mk = b.mark()
    TW = 256
    cm.alloc_ln(TW)
    x1s_r = b.rot("x1s", 1, [128, NCH, TW], F32)
    xt_r = b.rot("xt", 2, [128, NCH, TW + 16], F32)
    hp_r = b.rot("hp", 1, [128, NCH, TW + 16], F32)
    s_r = b.rot("ssum", 4, [128, TW + 16], F32)
    m_r = b.rot("msum", 3, [128, TW], F32)
    mx_r = b.rot("mixed", 2, [128, NCH, TW], BF16)
    u_r = b.rot("u1", 2, [128, NCH, TW], F32)
    yt_r = b.rot("ytmp", 2, [128, TW], F32)
    for (name, n, col) in segs:
        n1 = n + 1
        h2 = cm.h2[name]
        x1 = cm.x1[name]
        for (t0, T) in tiles_of(n1, TW):
            Wh = T + 16
            xt = xt_r.next()
            b.dma("sp", xt[:, :, 0:Wh], xin[name][:, :, t0:t0 + Wh].rearrange("c p t -> p c t"), (), [xt.r()])
            hp = hp_r.next()
            for c in range(NCH):
                b.act(hp[:, c, 0:Wh], xt[:, c, 0:Wh], AF.Identity, [xt.r(), der.r(), cm.modv.r()], [hp.r(c, c + 1)],
                      scale=cm.dv(0, c, col), bias=cm.mod(0, c, col))
            if t0 == 0:
                b.memset(hp[:, :, 0:8], 0.0, [hp.r()], eng="dve")
            mixed = mx_r.next()
            for c in range(NCH):
                w = POOL_WINDOWS[c // 2]
                cur = hp[:, c, :]
                curr = hp.r(c, c + 1)
                L = Wh
                step = 1
                while step < w:
                    s = s_r.next()
                    b.tt(s[:, 0:L - step], cur[:, 0:L - step], cur[:, step:L], ALU.add, [curr], [s.r()])
                    cur = s
                    curr = s.r()
                    L -= step
                    step *= 2
                k0 = 8 - w // 2
                m1 = m_r.next()
                m2 = m_r.next()
                b.ts(m1[:, 0:T], cur[:, k0 + 1:k0 + 1 + T], flags[:, 1:2], None, ALU.mult, None, [curr, flags.r()], [m1.r()])
                b.stt(m2[:, 0:T], cur[:, k0:k0 + T], flags[:, 0:1], m1[:, 0:T], ALU.mult, ALU.add,
                      [curr, flags.r(), m1.r()], [m2.r()])
                if t0 == 0:
                    b.tt(m2[:, 0:8], m2[:, 0:8], corr[:, c, :], ALU.mult, [m2.r(), corr.r()], [m2.r()])
                b.stt(mixed[:, c, 0:T], m2[:, 0:T], 1.0 / w, hp[:, c, 8:8 + T], ALU.mult, ALU.subtract,
                      [m2.r(), hp.r(c, c + 1)], [mixed.r(c, c + 1)])
            u = u_r.next()
            for dc in range(NCH):
                g = dc // 2
                dd = dc % 2
                pp = cm.pg.next()
                for cc_ in range(2):
                    b.mm(pp[:, 0:T], poolw[:, g, cc_, dd * 128:(dd + 1) * 128], mixed[:, 2 * g + cc_, 0:T],
                         cc_ == 0, cc_ == 1, [poolw.r(), mixed.r(2 * g + cc_, 2 * g + cc_ + 1)], [pp.r()])
                yt = yt_r.next()
                b.act(yt[:, 0:T], pp[:, 0:T], AF.Identity, [pp.r(), der.r()], [yt.r()],
                      scale=cm.dv(4, dc, col), bias=cm.dv(5, dc, col))
                b.stt(u[:, dc, 0:T], xt[:, dc, 8:8 + T], ALPHA, yt[:, 0:T], ALU.mult, ALU.add,
                      [xt.r(), yt.r()], [u.r()])
            cm.ln1_store(u, name, n, col, t0, T, x1s_r)
    b.release(mk)
    if stage == 1:
        dbg = b.dout("dbg", [NCH, 128, HALF])
        b.outs.append(b.dma("sp", dbg[:], cm.x1["l"][:], [cm.x1["l"].r()], []))
        dbg2 = b.dout("dbg2", [128, NCH * (HALF + 2)], BF16)
        b.outs.append(b.dma("sp", dbg2[:], cm.h2["l"][:].rearrange("p c t -> p (c t)"), [cm.h2["l"].r()], []))
        return b.finish()
    cm.emit_ffn(out_d, stage)
    return b.finish()


GM_NT = HALF + 128
GH = 2048


def build_layer3():
    b = Builder()
    segs = [("l", HALF, 0)]
    cm = Common(b, segs)
    xin = b.din("xin_l", [NCH, 128, GM_NT])
    out_d = {"l": b.dout("out_l", [NCH, 128, HALF])}
    win_d = b.din("g_w_in", [128, NCH, 2 * GH])
    bu_d = b.din("g_b_u", [128, 16])
    bv_d = b.din("g_b_v", [128, GH])
    lng_d = b.din("g_ln_g", [128, GH])
    lnb_d = b.din("g_ln_b", [128, GH])
    ws_d = b.din("g_wsT", [128, 8, 128])
    bs_d = b.din("g_bs", [128, 8, 128])
    wo_d = b.din("g_w_out", [128, 16, D])
    cm.emit_mods()
    cm.alloc_ffn({"l": GM_NT + 2})
    h1 = cm.h2["l"]
    UG = b.dscr("UGscr", [16, 128, GM_NT], BF16)
    bu = b.sb("g_bu", [128, 16], F32)
    b.dma("sp", bu[:], bu_d[:], (), [bu.r()])
    mk = b.mark()
    xt_r = b.rot("xt3", 2, [128, NCH, 512], F32)
    for (t0, T) in tiles_of(GM_NT, 512):
        xt = xt_r.next()
        b.dma("sp", xt[:, :, 0:T], xin[:, :, t0:t0 + T].rearrange("c p t -> p c t"), (), [xt.r()])
        for c in range(NCH):
            b.act(h1[:, c, 1 + t0:1 + t0 + T], xt[:, c, 0:T], AF.Identity, [xt.r(), cm.der.r(), cm.modv.r()], [h1.r()],
                  scale=cm.dv(0, c, 0), bias=cm.mod(0, c, 0))
    b.release(mk)
    mk = b.mark()
    wu_r = b.rot("wu", 3, [128, NCH, 128], BF16)
    ut_r = b.rot("ut", 3, [128, 512], BF16)
    for fc in range(16):
        wu = wu_r.next()
        b.dma("pool", wu[:], win_d[:, :, fc * 128:(fc + 1) * 128], (), [wu.r()])
        for (t0, T) in tiles_of(GM_NT, 512):
            pp = cm.pg.next()
            for kc in range(NCH):
                b.mm(pp[:, 0:T], wu[:, kc, :], h1[:, kc, 1 + t0:1 + t0 + T], kc == 0, kc == NCH - 1, [wu.r(), h1.r()], [pp.r()])
            ut = ut_r.next()
            b.act(ut[:, 0:T], pp[:, 0:T], AF.Gelu_apprx_tanh, [pp.r(), bu.r()], [ut.r()], bias=bu[:, fc:fc + 1])
            b.dma("sp", UG[fc, :, t0:t0 + T], ut[:, 0:T], [ut.r()], [UG.r(fc * GM_NT + t0, fc * GM_NT + t0 + T)])
    b.release(mk)
    mk = b.mark()
    wv = b.sb("wvv", [128, NCH, GH], BF16)
    for kc in range(NCH):
        b.dma("pool", wv[:, kc, :], win_d[:, kc, GH:2 * GH], (), [wv.r(kc, kc + 1)])
    wo = b.sb("wo3", [128, 16, D], BF16)
    for fc in range(16):
        b.dma("pool", wo[:, fc, :], wo_d[:, fc, :], (), [wo.r(fc, fc + 1)])
    bvb = b.sb("bvb", [128, GH], BF16)
    b.dma("pool", bvb[:], bv_d[:], (), [bvb.r()])
    lng = b.sb("lng3", [128, GH], F32)
    b.dma("sp", lng[:], lng_d[:], (), [lng.r()])
    lnb = b.sb("lnb3", [128, GH], F32)
    b.dma("sp", lnb[:], lnb_d[:], (), [lnb.r()])
    ws = b.sb("ws3", [128, 8, 128], BF16)
    b.dma("pool", ws[:], ws_d[:], (), [ws.r()])
    bs = b.sb("bs3", [128, 8, 128], F32)
    b.dma("sp", bs[:], bs_d[:], (), [bs.r()])
    cm.alloc_ln(128)
    v_r = b.rot("v3", 1, [128, GH], F32)
    v2_r = b.rot("v3b", 1, [128, GH], F32)
    vn_r = b.rot("vn3", 1, [128, GH], BF16)
    ugl_r = b.rot("ugl", 1, [128, 16, 128], BF16)
    ug_r = b.rot("ug", 1, [128, 16, 128], BF16)
    gt_r = b.rot("gtm", 2, [128, 128], F32)
    xt_r = b.rot("xt3b", 2, [128, NCH, 128], F32)
    u_r = b.rot("u3", 1, [128, NCH, 128], F32)
    x1s_r = b.rot("x1s3", 1, [128, NCH, 128], F32)
    st_r = b.rot("vst", 2, [128, 16], F32)
    yt_r = b.rot("yt3", 2, [128, 128], F32)
    for ck in range(GM_NT // 128):
        t0 = ck * 128
        v = v_r.next()
        st = st_r.next()
        for cb in range(4):
            pp = cm.pg.next()
            for kc in range(NCH):
                b.mm(pp[:, 0:512], h1[:, kc, 1 + t0:1 + t0 + 128], wv[:, kc, cb * 512:(cb + 1) * 512], kc == 0, kc == NCH - 1,
                     [h1.r(), wv.r(kc, kc + 1)], [pp.r()])
            b.tt(v[:, cb * 512:(cb + 1) * 512], pp[:, 0:512], bvb[:, cb * 512:(cb + 1) * 512], ALU.add,
                 [pp.r(), bvb.r()], [v.r()])
            b.S.add("act", lambda e, v=v, cb=cb, st=st: e.activation(
                out=v[:, cb * 512:(cb + 1) * 512], in_=v[:, cb * 512:(cb + 1) * 512], func=AF.Gelu_apprx_tanh,
                accum_out=st[:, cb:cb + 1]), [v.r()], [v.r(), st.r()])
        v2 = v2_r.next()
        b.S.add("act", lambda e, v=v, v2=v2, st=st: e.activation(out=v2[:], in_=v[:], func=AF.Square,
                                                               accum_out=st[:, 4:5]), [v.r()], [v2.r(), st.r()])
        RW = [st.r()]
        b.tt(st[:, 5:6], st[:, 0:1], st[:, 1:2], ALU.add, RW, RW)
        b.tt(st[:, 6:7], st[:, 2:3], st[:, 3:4], ALU.add, RW, RW)
        b.tt(st[:, 5:6], st[:, 5:6], st[:, 6:7], ALU.add, RW, RW)
        b.ts(st[:, 6:7], st[:, 5:6], 1.0 / GH, None, ALU.mult, None, RW, RW)
        b.tt(st[:, 7:8], st[:, 6:7], st[:, 6:7], ALU.mult, RW, RW)
        b.stt(st[:, 8:9], st[:, 4:5], 1.0 / GH, st[:, 7:8], ALU.mult, ALU.subtract, RW, RW)
        b.act(st[:, 8:9], st[:, 8:9], AF.Sqrt, RW + [cm.epsv.r()], RW, bias=cm.epsv[:, 0:1])
        b.S.add("dve", lambda e, st=st: e.reciprocal(out=st[:, 8:9], in_=st[:, 8:9]), RW, RW)
        b.stt(st[:, 9:10], st[:, 6:7], -1.0, st[:, 8:9], ALU.mult, ALU.mult, RW, RW)
        b.act(v2[:], v[:], AF.Identity, [v.r(), st.r()], [v2.r()], scale=st[:, 8:9], bias=st[:, 9:10])
        b.tt(v2[:], v2[:], lng[:], ALU.mult, [v2.r(), lng.r()], [v2.r()])
        vn = vn_r.next()
        b.tt(vn[:], v2[:], lnb[:], ALU.add, [v2.r(), lnb.r()], [vn.r()], eng="pool")
        ugl = ugl_r.next()
        b.dma("sp", ugl[:], UG[:, :, t0:t0 + 128].rearrange("f p t -> p f t"),
              [UG.r(fc * GM_NT + t0, fc * GM_NT + t0 + 128) for fc in range(16)], [ugl.r()])
        ug = ug_r.next()
        for fc in range(16):
            hh = fc // 2
            pp = cm.pg.next()
            b.mm(pp[:, 0:128], vn[:, fc * 128:(fc + 1) * 128], ws[:, hh, :], True, True, [vn.r(), ws.r()], [pp.r()])
            gt = gt_r.next()
            b.tt(gt[:], pp[:, 0:128], bs[:, hh, :], ALU.add, [pp.r(), bs.r()], [gt.r()])
            b.tt(ug[:, fc, :], gt[:], ugl[:, fc, :], ALU.mult, [gt.r(), ugl.r()], [ug.r(fc, fc + 1)], eng="pool")
        xt = xt_r.next()
        b.dma("sp", xt[:], xin[:, :, t0:t0 + 128].rearrange("c p t -> p c t"), (), [xt.r()])
        u = u_r.next()
        for dc in range(NCH):
            pp = cm.pg.next()
            for fc in range(16):
                b.mm(pp[:, 0:128], wo[:, fc, dc * 128:(dc + 1) * 128], ug[:, fc, :], fc == 0, fc == 15,
                     [wo.r(fc, fc + 1), ug.r(fc, fc + 1)], [pp.r()])
            yt = yt_r.next()
            b.act(yt[:], pp[:, 0:128], AF.Identity, [pp.r(), cm.modv.r()], [yt.r()], scale=cm.mod(2, dc, 0))
            b.stt(u[:, dc, :], xt[:, dc, :], ALPHA, yt[:], ALU.mult, ALU.add, [xt.r(), yt.r()], [u.r()])
        cm.ln1_store(u, "l", HALF, 0, t0, 128, x1s_r)
    b.release(mk)
    b.memset(cm.h2["l"][:, :, 0:1], 0.0, [cm.h2["l"].r()])
    cm.emit_ffn(out_d)
    return b.finish()


SS_NT = CTX + SEQ
TWO_PI = 6.283185307179586


def build_ssm_scan():
    b = Builder()
    cm = Common(b, [], ffn=False)
    xs = b.din("xs", [NCH, 128, SS_NT])
    yo = b.dout("yo", [NCH, 128, SEQ])
    lam_d = b.din("s_lam", [128, 3, 32])
    bt_d = b.din("s_bt", [128, 2, 32, 128])
    ct_d = b.din("s_ct", [128, 2, 32, 32])
    jj_d = b.din("s_jj", [128, 128])
    cm.emit_mods()
    V = lambda n: b.sb(n, [128, 32], F32)
    lam = b.sb("lam", [128, 3, 32], F32)
    b.dma("sp", lam[:], lam_d[:], (), [lam.r()])
    jj = b.sb("jj", [128, 128], F32)
    b.dma("sp", jj[:], jj_d[:], (), [jj.r()])
    bt = b.sb("bt", [128, 2, 32, 128], BF16)
    b.dma("pool", bt[:, 0], bt_d[:, 0], (), [bt.r()])
    b.dma("pool", bt[:, 1], bt_d[:, 1], (), [bt.r()])
    ct = b.sb("ct", [128, 2, 32, 32], F32)
    b.dma("sp", ct[:], ct_d[:], (), [ct.r()])
    dt, ar, ai, rr, lbr, lbi = V("dt"), V("ar"), V("ai"), V("rr"), V("lbr"), V("lbi")
    sn, cs, t1v, t2v, den, cre, cim = V("sn"), V("cs"), V("t1v"), V("t2v"), V("den"), V("cre"), V("cim")
    ki = b.sb("ki", [128, 32], mybir.dt.int32)
    negpi = b.sb("negpi", [128, 1], F32)
    b.memset(negpi[:], -3.141592653589793, [negpi.r()])

    def sincos(dst_sin, dst_cos, ang, tmp, kint, shape_r):
        for (dst, shift) in ((dst_sin, 0.0), (dst_cos, 1.5707963267948966)):
            b.ts(tmp, ang, shift, None, ALU.add, None, shape_r, shape_r)
            b.ts(kint, tmp, 1.0 / TWO_PI, None, ALU.mult, None, shape_r, shape_r)
            b.copy(dst, kint, shape_r, shape_r)
            b.stt(tmp, dst, -TWO_PI, tmp, ALU.mult, ALU.add, shape_r, shape_r)
            b.act(dst, tmp, AF.Sin, shape_r, shape_r)

    RS = [lam.r(), dt.r(), ar.r(), ai.r(), rr.r(), lbr.r(), lbi.r(), sn.r(), cs.r(), t1v.r(), t2v.r(), den.r(),
          cre.r(), cim.r(), ki.r()]
    b.act(dt[:], lam[:, 2, :], AF.Exp, RS, RS)
    b.tt(ar[:], lam[:, 0, :], dt[:], ALU.mult, RS, RS)
    b.tt(ai[:], lam[:, 1, :], dt[:], ALU.mult, RS, RS)
    b.act(rr[:], ar[:], AF.Exp, RS, RS)
    sincos(sn[:], cs[:], ai[:], t1v[:], ki[:], RS)
    b.tt(lbr[:], rr[:], cs[:], ALU.mult, RS, RS)
    b.tt(lbi[:], rr[:], sn[:], ALU.mult, RS, RS)
    b.ts(t1v[:], lbr[:], -1.0, None, ALU.add, None, RS, RS)
    b.tt(den[:], lam[:, 0, :], lam[:, 0, :], ALU.mult, RS, RS)
    b.tt(t2v[:], lam[:, 1, :], lam[:, 1, :], ALU.mult, RS, RS)
    b.tt(den[:], den[:], t2v[:], ALU.add, RS, RS)
    b.S.add("dve", lambda e: e.reciprocal(out=den[:], in_=den[:]), RS, RS)
    b.tt(cre[:], t1v[:], lam[:, 0, :], ALU.mult, RS, RS)
    b.tt(t2v[:], lbi[:], lam[:, 1, :], ALU.mult, RS, RS)
    b.tt(cre[:], cre[:], t2v[:], ALU.add, RS, RS)
    b.tt(cre[:], cre[:], den[:], ALU.mult, RS, RS)
    b.tt(cim[:], lbi[:], lam[:, 0, :], ALU.mult, RS, RS)
    b.tt(t2v[:], t1v[:], lam[:, 1, :], ALU.mult, RS, RS)
    b.tt(cim[:], cim[:], t2v[:], ALU.subtract, RS, RS)
    b.tt(cim[:], cim[:], den[:], ALU.mult, RS, RS)
    cpr = b.sb("cpr", [128, 32, 128], BF16)
    cpi = b.sb("cpi", [128, 32, 128], BF16)
    mk_c = b.mark()
    ctr = b.sb("ctr", [128, 32, 32], F32)
    cti = b.sb("cti", [128, 32, 32], F32)
    ctt = b.sb("ctt", [128, 32, 32], F32)
    RC = [ct.r(), ctr.r(), cti.r(), ctt.r(), cre.r(), cim.r()]
    bc = lambda v: v[:].unsqueeze(2).broadcast_to([128, 32, 32])
    b.tt(ctr[:], ct[:, 0], bc(cre), ALU.mult, RC, RC)
    b.tt(ctt[:], ct[:, 1], bc(cim), ALU.mult, RC, RC)
    b.tt(ctr[:], ctr[:], ctt[:], ALU.subtract, RC, RC)
    b.tt(cti[:], ct[:, 0], bc(cim), ALU.mult, RC, RC)
    b.tt(ctt[:], ct[:, 1], bc(cre), ALU.mult, RC, RC)
    b.tt(cti[:], cti[:], ctt[:], ALU.add, RC, RC)
    b.memset(cpr[:], 0.0, [cpr.r()])
    b.memset(cpi[:], 0.0, [cpi.r()])
    for q in range(32):
        r_ = q % 4
        b.copy(cpr[:, q, r_ * 32:(r_ + 1) * 32], ctr[:, q, :], RC, [cpr.r()])
        b.ts(cpi[:, q, r_ * 32:(r_ + 1) * 32], cti[:, q, :], -1.0, None, ALU.mult, None, RC, [cpi.r()])
    b.release(mk_c)
    BIGS = [128, 32, 128]
    cosT = b.sb("cosT", BIGS, F32)
    sinT = b.sb("sinT", BIGS, F32)
    Rm = b.sb("Rm", BIGS, F32)
    A = b.sb("bigA", BIGS, F32)
    Bq = b.sb("bigB", BIGS, F32)
    T1 = b.sb("bigT1", BIGS, F32)
    T2 = b.sb("bigT2", BIGS, F32)
    KI = Tl(None, "sb", T2.ap.bitcast(mybir.dt.int32), base=T2.base, unit=T2.unit, n1=T2.n1)
    RT = [cosT.r(), sinT.r(), A.r(), T1.r(), KI.r(), ai.r(), jj.r()]
    b.tt(A[:], jj[:].unsqueeze(1).broadcast_to(BIGS), ai[:].unsqueeze(2).broadcast_to(BIGS), ALU.mult, RT, RT)
    sincos(sinT[:], cosT[:], A[:], T1[:], KI[:], RT)
    b.copy(Rm[:], rr[:].unsqueeze(2).broadcast_to(BIGS), [rr.r()], [Rm.r()])
    b.memset(Rm[:, :, 0:1], 0.0, [Rm.r()], eng="dve")
    sre = b.sb("sre", BIGS, BF16)
    sim = b.sb("sim", BIGS, BF16)
    car = b.sb("carry", [128, 4, 32], F32)
    b.memset(car[:], 0.0, [car.r()])
    h_r = b.rot("hss", 2, [128, NCH, 128], BF16)
    xt_r = b.rot("xss", 2, [128, NCH, 128], F32)
    y_r = b.rot("yss", 2, [128, NCH, 128], F32)
    f2 = lambda t: t[:].rearrange("p a b -> p (a b)")
    for ck in range(SS_NT // 128):
        t0 = ck * 128
        col = 1 if ck < 2 else 0
        xt = xt_r.next()
        b.dma("sp", xt[:], xs[:, :, t0:t0 + 128].rearrange("c p t -> p c t"), (), [xt.r()])
        h = h_r.next()
        for c in range(NCH):
            b.act(h[:, c, :], xt[:, c, :], AF.Identity, [xt.r(), cm.der.r(), cm.modv.r()], [h.r()],
                  scale=cm.dv(0, c, col), bias=cm.mod(0, c, col))
        for (ri, dst) in ((0, A), (1, Bq)):
            for g4 in range(8):
                pp = cm.pg.next()
                for r_ in range(4):
                    q = g4 * 4 + r_
                    b.mm(pp[:, r_ * 128:(r_ + 1) * 128], bt[:, ri, q, :], h[:, g4, :], True, True,
                         [bt.r(), h.r()], [pp.r()])
                b.act(dst[:, g4 * 4:(g4 + 1) * 4, :], pp[:].rearrange("p (a b) -> p a b", a=4), AF.Copy,
                      [pp.r()], [dst.r()])
        b.tt(T1[:], cosT[:], A[:], ALU.mult, [cosT.r(), A.r()], [T1.r()])
        b.tt(T2[:], sinT[:], Bq[:], ALU.mult, [sinT.r(), Bq.r()], [T2.r()], eng="pool")
        b.tt(T1[:], T1[:], T2[:], ALU.add, [T1.r(), T2.r()], [T1.r()])
        b.tt(A[:], sinT[:], A[:], ALU.mult, [sinT.r(), A.r()], [A.r()], eng="pool")
        b.tt(Bq[:], cosT[:], Bq[:], ALU.mult, [cosT.r(), Bq.r()], [Bq.r()])
        b.tt(Bq[:], Bq[:], A[:], ALU.subtract, [Bq.r(), A.r()], [Bq.r()])
        RCc = [car.r(), lbr.r(), lbi.r()]
        b.tt(car[:, 2], car[:, 0], lbr[:], ALU.mult, RCc, RCc)
        b.tt(car[:, 3], car[:, 1], lbi[:], ALU.mult, RCc, RCc)
        b.tt(car[:, 2], car[:, 2], car[:, 3], ALU.subtract, RCc, RCc)
        b.tt(T1[:, :, 0], T1[:, :, 0], car[:, 2], ALU.add, [T1.r(), car.r()], [T1.r()])
        b.tt(car[:, 2], car[:, 0], lbi[:], ALU.mult, RCc, RCc)
        b.tt(car[:, 3], car[:, 1], lbr[:], ALU.mult, RCc, RCc)
        b.tt(car[:, 2], car[:, 2], car[:, 3], ALU.add, RCc, RCc)
        b.tt(Bq[:, :, 0], Bq[:, :, 0], car[:, 2], ALU.add, [Bq.r(), car.r()], [Bq.r()])
        b.S.add("dve", lambda e: e.tensor_tensor_scan(out=f2(T2), data0=f2(Rm), data1=f2(T1), initial=0.0,
                                                      op0=ALU.mult, op1=ALU.add), [Rm.r(), T1.r()], [T2.r()])
        b.S.add("dve", lambda e: e.tensor_tensor_scan(out=f2(A), data0=f2(Rm), data1=f2(Bq), initial=0.0,
                                                      op0=ALU.mult, op1=ALU.add), [Rm.r(), Bq.r()], [A.r()])
        b.tt(T1[:], cosT[:], T2[:], ALU.mult, [cosT.r(), T2.r()], [T1.r()])
        b.tt(Bq[:], sinT[:], A[:], ALU.mult, [sinT.r(), A.r()], [Bq.r()], eng="pool")
        b.tt(car[:, 0], T1[:, :, 127], Bq[:, :, 127], ALU.subtract, [T1.r(), Bq.r()], [car.r()])
        b.tt(sre[:], T1[:], Bq[:], ALU.subtract, [T1.r(), Bq.r()], [sre.r()])
        b.tt(T1[:], sinT[:], T2[:], ALU.mult, [sinT.r(), T2.r()], [T1.r()], eng="pool")
        b.tt(Bq[:], cosT[:], A[:], ALU.mult, [cosT.r(), A.r()], [Bq.r()])
        b.tt(car[:, 1], T1[:, :, 127], Bq[:, :, 127], ALU.add, [T1.r(), Bq.r()], [car.r()])
        b.tt(sim[:], T1[:], Bq[:], ALU.add, [T1.r(), Bq.r()], [sim.r()], eng="pool")
        if ck < 2:
            continue
        y = y_r.next()
        for g4 in range(8):
            pp = cm.pg.next()
            k = 0
            for r_ in range(4):
                q = g4 * 4 + r_
                for (cw, sw) in ((cpr, sre), (cpi, sim)):
                    b.mm(pp[:, 0:128], cw[:, q, :], sw[:, q, :], k == 0, k == 7, [cw.r(), sw.r()], [pp.r()])
                    k += 1
            b.act(y[:, g4, :], pp[:, 0:128], AF.Copy, [pp.r()], [y.r()])
        op = b.dma("sp", yo[:, :, t0 - CTX:t0 - CTX + 128].rearrange("c p t -> p c t"), y[:], [y.r()], [])
        b.outs.append(op)
    return b.finish()


def build_layer2b():
    b = Builder()
    segs = [("l", HALF, 0)]
    cm = Common(b, segs)
    NT = HALF + 1
    xin = b.din("xin_l", [NCH, 128, NT])
    yf_d = b.din("yf", [NCH, 128, NT])
    yr_d = b.din("yr", [NCH, 128, NT])
    out_d = {"l": b.dout("out_l", [NCH, 128, HALF])}
    dsk_d = b.din("s_d", [128, NCH])
    wa_d = b.din("s_wa", [128, NCH, D])
    wb_d = b.din("s_wb", [128, NCH, D])
    cm.emit_mods()
    cm.alloc_ffn()
    mk = b.mark()
    dsk = b.sb("dsk", [128, NCH], F32)
    b.dma("sp", dsk[:], dsk_d[:], (), [dsk.r()])
    wa = b.sb("wa", [128, NCH, D], BF16)
    wb = b.sb("wb", [128, NCH, D], BF16)
    for kc in range(NCH):
        b.dma("pool", wa[:, kc, :], wa_d[:, kc, :], (), [wa.r(kc, kc + 1)])
        b.dma("pool", wb[:, kc, :], wb_d[:, kc, :], (), [wb.r(kc, kc + 1)])
    TW = 256
    cm.alloc_ln(TW)
    xt_r = b.rot("xt2", 2, [128, NCH, TW], F32)
    yf_r = b.rot("yft", 2, [128, NCH, TW], F32)
    yr_r = b.rot("yrt", 2, [128, NCH, TW], F32)
    hh_r = b.rot("hh2", 1, [128, NCH, TW], F32)
    g_r = b.rot("g2", 2, [128, NCH, TW], BF16)
    sg_r = b.rot("sg2", 2, [128, TW], F32)
    yt_r = b.rot("yt2", 2, [128, TW], F32)
    u_r = b.rot("u2b", 1, [128, NCH, TW], F32)
    x1s_r = b.rot("x1s2", 1, [128, NCH, TW], F32)
    for (t0, T) in tiles_of(NT, TW):
        xt, yft, yrt = xt_r.next(), yf_r.next(), yr_r.next()
        for (dst, src) in ((xt, xin), (yft, yf_d), (yrt, yr_d)):
            b.dma("sp", dst[:, :, 0:T], src[:, :, t0:t0 + T].rearrange("c p t -> p c t"), (), [dst.r()])
        hh = hh_r.next()
        g = g_r.next()
        for c in range(NCH):
            b.act(hh[:, c, 0:T], xt[:, c, 0:T], AF.Identity, [xt.r(), cm.der.r(), cm.modv.r()], [hh.r(c, c + 1)],
                  scale=cm.dv(0, c, 0), bias=cm.mod(0, c, 0))
            b.stt(hh[:, c, 0:T], hh[:, c, 0:T], dsk[:, c:c + 1], yft[:, c, 0:T], ALU.mult, ALU.add,
                  [hh.r(c, c + 1), dsk.r(), yft.r()], [hh.r(c, c + 1)])
            b.tt(hh[:, c, 0:T], hh[:, c, 0:T], yrt[:, c, 0:T], ALU.add, [hh.r(c, c + 1), yrt.r()], [hh.r(c, c + 1)],
                 eng="pool")
            b.act(g[:, c, 0:T], hh[:, c, 0:T], AF.Gelu_apprx_tanh, [hh.r(c, c + 1)], [g.r(c, c + 1)])
        u = u_r.next()
        for dc in range(NCH):
            pa_ = cm.pg.next()
            pb_ = cm.pg.next()
            for kc in range(NCH):
                b.mm(pa_[:, 0:T], wa[:, kc, dc * 128:(dc + 1) * 128], g[:, kc, 0:T], kc == 0, kc == NCH - 1,
                     [wa.r(kc, kc + 1), g.r(kc, kc + 1)], [pa_.r()])
            for kc in range(NCH):
                b.mm(pb_[:, 0:T], wb[:, kc, dc * 128:(dc + 1) * 128], g[:, kc, 0:T], kc == 0, kc == NCH - 1,
                     [wb.r(kc, kc + 1), g.r(kc, kc + 1)], [pb_.r()])
            sg = sg_r.next()
            b.act(sg[:, 0:T], pb_[:, 0:T], AF.Sigmoid, [pb_.r()], [sg.r()])
            yt = yt_r.next()
            b.stt(yt[:, 0:T], pa_[:, 0:T], cm.mod(2, dc, 0), sg[:, 0:T], ALU.mult, ALU.mult,
                  [pa_.r(), cm.modv.r(), sg.r()], [yt.r()])
            b.stt(u[:, dc, 0:T], xt[:, dc, 0:T], ALPHA, yt[:, 0:T], ALU.mult, ALU.add, [xt.r(), yt.r()], [u.r()])
        cm.ln1_store(u, "l", HALF, 0, t0, T, x1s_r)
    b.release(mk)
    cm.emit_ffn(out_d)
    return b.finish()


AT_NQ = HALF + 128
AT_NK = HALF + 256
NEG = -30000.0


def build_layer1():
    b = Builder()
    segs = [("l", HALF, 0), ("c", CHALF, 1)]
    cm = Common(b, segs)
    xin = {"l": b.din("xin_l", [NCH, 128, AT_NK]), "c": b.din("xin_c", [NCH, 128, CTX])}
    out_d = {"l": b.dout("out_l", [NCH, 128, HALF]), "c": b.dout("out_c", [NCH, 128, CHALF])}
    wq_d = b.din("a_wq", [128, NCH, 16, 128])
    wqr_d = b.din("a_wqr", [128, NCH, 16, 128])
    wv_d = b.din("a_wv", [128, NCH, 256])
    wo_d = b.din("a_wo", [64, 16, D])
    cos_d = b.din("a_cos", [128, AT_NK])
    sin_d = b.din("a_sin", [128, AT_NK])
    mask_d = b.din("a_mask", [128, 2, 512])
    sink_d = b.din("a_sink", [64, 16, 128])
    ident_d = b.din("a_ident", [128, 128])
    cm.emit_mods()
    cm.alloc_ffn({"l": AT_NK + 2, "c": CTX + 2})
    NT = {"l": AT_NK, "c": CTX}
    h1 = cm.h2
    mk0 = b.mark()
    qT = {"l": b.dscr("qT_l", [NCH, 128, AT_NQ], BF16), "c": b.sb("qT_c", [128, NCH, CTX], BF16)}
    kT = {"l": b.sb("kT_l", [128, 8, AT_NK], BF16), "c": b.sb("kT_c", [128, 8, CTX], BF16)}
    Vt = {"l": b.sb("V_l", [128, AT_NK // 128, 256], BF16), "c": b.sb("V_c", [128, 2, 256], BF16)}
    mk = b.mark()
    xt_r = b.rot("xt1", 2, [128, NCH, 512], F32)
    for (name, n, col) in segs:
        for (t0, T) in tiles_of(NT[name], 512):
            xt = xt_r.next()
            b.dma("sp", xt[:, :, 0:T], xin[name][:, :, t0:t0 + T].rearrange("c p t -> p c t"), (), [xt.r()])
            for c in range(NCH):
                b.act(h1[name][:, c, 1 + t0:1 + t0 + T], xt[:, c, 0:T], AF.Identity,
                      [xt.r(), cm.der.r(), cm.modv.r()], [h1[name].r()],
                      scale=cm.dv(0, c, col), bias=cm.mod(0, c, col))
    b.release(mk)
    mk = b.mark()
    cosT = b.sb("cosT1", [128, AT_NK], F32)
    sinT = b.sb("sinT1", [128, AT_NK], F32)
    b.dma("sp", cosT[:], cos_d[:], (), [cosT.r()])
    b.dma("sp", sinT[:], sin_d[:], (), [sinT.r()])
    w_r = b.rot("wq1", 3, [128, NCH, 128], BF16)
    wr_r = b.rot("wqr1", 3, [128, NCH, 128], BF16)
    ra_r = b.rot("ropea", 2, [128, 512], F32)
    rb_r = b.rot("ropeb", 2, [128, 512], F32)
    qo_r = b.rot("qo1", 2, [128, 512], BF16)
    for oc in range(16):
        w = w_r.next()
        wr = wr_r.next()
        b.dma("pool", w[:], wq_d[:, :, oc, :], (), [w.r()])
        b.dma("pool", wr[:], wqr_d[:, :, oc, :], (), [wr.r()])
        neg_view = wr[:].rearrange("p k (a two s) -> p k a two s", two=2, s=16)[:, :, :, 0, :]
        b.ts(neg_view, neg_view, -1.0, None, ALU.mult, None, [wr.r()], [wr.r()])
        isq = oc < 8
        dstl = qT["l"] if isq else kT["l"]
        dstc = qT["c"] if isq else kT["c"]
        oi = oc if isq else oc - 8
        nl = AT_NQ if isq else AT_NK
        for (t0, T) in tiles_of(nl, 512):
            pa_ = cm.pg.next()
            pb_ = cm.pg.next()
            for kc in range(NCH):
                b.mm(pa_[:, 0:T], w[:, kc, :], h1["l"][:, kc, 1 + t0:1 + t0 + T], kc == 0, kc == NCH - 1,
                     [w.r(), h1["l"].r()], [pa_.r()])
            for kc in range(NCH):
                b.mm(pb_[:, 0:T], wr[:, kc, :], h1["l"][:, kc, 1 + t0:1 + t0 + T], kc == 0, kc == NCH - 1,
                     [wr.r(), h1["l"].r()], [pb_.r()])
            ra = ra_r.next()
            rb = rb_r.next()
            b.tt(ra[:, 0:T], pa_[:, 0:T], cosT[:, t0:t0 + T], ALU.mult, [pa_.r(), cosT.r()], [ra.r()])
            b.tt(rb[:, 0:T], pb_[:, 0:T], sinT[:, t0:t0 + T], ALU.mult, [pb_.r(), sinT.r()], [rb.r()])
            if isq:
                qo = qo_r.next()
                b.tt(qo[:, 0:T], ra[:, 0:T], rb[:, 0:T], ALU.add, [ra.r(), rb.r()], [qo.r()], eng="pool")
                b.dma("sp", dstl[oi, :, t0:t0 + T], qo[:, 0:T], [qo.r()], [dstl.r()])
            else:
                b.tt(dstl[:, oi, t0:t0 + T], ra[:, 0:T], rb[:, 0:T], ALU.add, [ra.r(), rb.r()], [dstl.r()], eng="pool")
        pc_ = cm.pg.next()
        for kc in range(NCH):
            b.mm(pc_[:, 0:CTX], w[:, kc, :], h1["c"][:, kc, 1:1 + CTX], kc == 0, kc == NCH - 1,
                 [w.r(), h1["c"].r()], [pc_.r()])
        b.act(dstc[:, oi, :], pc_[:, 0:CTX], AF.Copy, [pc_.r()], [dstc.r()])
    wv = b.sb("wv1", [128, NCH, 256], BF16)
    b.dma("pool", wv[:], wv_d[:], (), [wv.r()])
    for (name, n, col) in segs:
        for blk in range(NT[name] // 128):
            pv_ = cm.pg.next()
            for kc in range(NCH):
                b.mm(pv_[:, 0:256], h1[name][:, kc, 1 + blk * 128:1 + (blk + 1) * 128], wv[:, kc, :], kc == 0,
                     kc == NCH - 1, [h1[name].r(), wv.r()], [pv_.r()])
            b.act(Vt[name][:, blk, :], pv_[:, 0:256], AF.Copy, [pv_.r()], [Vt[name].r()])
    b.release(mk)
    import os
    if os.environ.get("DBG_A") == "1":
        dbg = b.dout("dbg", [128, NCH * AT_NQ], BF16)
        b.outs.append(b.dma("sp", dbg[:].rearrange("p (c t) -> c p t", c=NCH), qT["l"][:], [qT["l"].r()], []))
        dbg2 = b.dout("dbg2", [128, 8 * AT_NK], BF16)
        b.outs.append(b.dma("sp", dbg2[:], kT["l"][:].rearrange("p c t -> p (c t)"), [kT["l"].r()], []))
        dbg3 = b.dout("dbg3", [128, 18 * 256], BF16)
        b.outs.append(b.dma("sp", dbg3[:], Vt["l"][:].rearrange("p c t -> p (c t)"), [Vt["l"].r()], []))
        return b.finish()
    mk = b.mark()
    wo = b.sb("wo1", [64, 16, D], BF16)
    for hq in range(16):
        b.dma("pool", wo[:, hq, :], wo_d[:, hq, :], (), [wo.r()])
    maskt = b.sb("mask1", [128, 2, 512], BF16)
    b.dma("pool", maskt[:], mask_d[:], (), [maskt.r()])
    ident = b.sb("ident1", [128, 128], BF16)
    b.dma("pool", ident[:], ident_d[:], (), [ident.r()])
    onesb = b.sb("onesb", [128, 64], BF16)
    b.memset(onesb[:], 1.0, [onesb.r()])
    sinke = b.sb("sinke", [64, 16, 128], F32)
    b.dma("sp", sinke[:], sink_d[:], (), [sinke.r()])
    b.act(sinke[:], sinke[:], AF.Exp, [sinke.r()], [sinke.r()])
    cm.alloc_ln(128)
    pt_r = b.rot("pt1", 3, [128, 512], BF16)
    den_r = b.rot("den1", 2, [64, 512], F32)
    ao_r = b.rot("ao1", 2, [64, 16, 128], BF16)
    xq_r = b.rot("xq1", 2, [128, NCH, 128], F32)
    qb_r = b.rot("qb1", 2, [128, NCH, 128], BF16)
    u_r = b.rot("u1a", 1, [128, NCH, 128], F32)
    yt_r = b.rot("yt1", 2, [128, 128], F32)
    x1s_r = b.rot("x1s1", 1, [128, NCH, 128], F32)
    pab = Rot(cm.pab)
    if os.environ.get("DBG_A") == "3":
        dbg_y = b.dout("dbg_y", [NCH, 128, AT_NQ])
        ydbg_r = b.rot("ydbg", 2, [128, 128], F32)
    for (name, n, col) in segs:
        nqb = (AT_NQ if name == "l" else CTX) // 128
        for nb in range(nqb):
            ao = ao_r.next()
            if name == "l":
                qb = qb_r.next()
                b.dma("sp", qb[:], qT["l"][:, :, nb * 128:(nb + 1) * 128].rearrange("c p t -> p c t"), [qT["l"].r()], [qb.r()])
                qsrc = lambda j, qb=qb: qb[:, j, :]
                qres = qb.r()
            else:
                qsrc = lambda j, nb=nb: qT["c"][:, j, nb * 128:(nb + 1) * 128]
                qres = qT["c"].r()
            for hk in range(4):
                kcs = []
                if name == "l":
                    if nb >= 1:
                        kcs.append((kT["l"], Vt["l"], nb - 1, 0))
                    kcs.append((kT["l"], Vt["l"], nb, None))
                    kcs.append((kT["l"], Vt["l"], nb + 1, 1))
                kcs.append((kT["c"], Vt["c"], 0, None))
                kcs.append((kT["c"], Vt["c"], 1, None))
                po = pab.next()
                pd = pab.next()
                for ki_, (kt, vt, kb, mi) in enumerate(kcs):
                    ps_ = cm.pg.next()
                    if mi is not None:
                        b.mm(ps_[:, 0:512], ident[:], maskt[:, mi, :], True, False, [ident.r(), maskt.r()], [ps_.r()])
                    for r_ in range(4):
                        hq = 4 * hk + r_
                        j = hq // 2
                        b.mm(ps_[:, r_ * 128:(r_ + 1) * 128], kt[:, 2 * hk + (hq % 2), kb * 128:(kb + 1) * 128],
                             qsrc(j), mi is None, (mi is None) or r_ == 3, [kt.r(), qres], [ps_.r()])
                    pt = pt_r.next()
                    b.act(pt[:], ps_[:, 0:512], AF.Exp, [ps_.r()], [pt.r()], scale=0.125)
                    last = ki_ == len(kcs) - 1
                    b.mm(po[0:64, 0:512], vt[:, kb, hk * 64:(hk + 1) * 64], pt[:], ki_ == 0, last,
                         [vt.r(), pt.r()], [po.r()])
                    b.mm(pd[0:64, 0:512], onesb[:], pt[:], ki_ == 0, last, [onesb.r(), pt.r()], [pd.r()])
                den = den_r.next()
                b.tt(den[:], pd[0:64, 0:512], sinke[:, 4 * hk:4 * hk + 4, :].rearrange("p a b -> p (a b)"), ALU.add,
                     [pd.r(), sinke.r()], [den.r()])
                b.S.add("dve", lambda e, den=den: e.reciprocal(out=den[:], in_=den[:]), [den.r()], [den.r()])
                b.tt(ao[:, 4 * hk:4 * hk + 4, :].rearrange("p a b -> p (a b)"), po[0:64, 0:512], den[:], ALU.mult,
                     [po.r(), den.r()], [ao.r()])
            t0 = nb * 128
            if os.environ.get("DBG_A") == "3" and name == "l" and nb == 3:
                dbg_ao = b.dout("dbg_ao", [64, 16 * 128], BF16)
                b.outs.append(b.dma("sp", dbg_ao[:], ao[:].rearrange("p a b -> p (a b)"), [ao.r()], []))
            xq = xq_r.next()
            b.dma("sp", xq[:], xin[name][:, :, t0:t0 + 128].rearrange("c p t -> p c t"), (), [xq.r()])
            u = u_r.next()
            for dc in range(NCH):
                pp = cm.pg.next()
                for hq in range(16):
                    b.mm(pp[:, 0:128], wo[:, hq, dc * 128:(dc + 1) * 128], ao[:, hq, :], hq == 0, hq == 15,
                         [wo.r(), ao.r()], [pp.r()])
                yt = yt_r.next()
                if os.environ.get("DBG_A") == "3" and name == "l":
                    yd = ydbg_r.next()
                    b.act(yd[:], pp[:, 0:128], AF.Copy, [pp.r()], [yd.r()])
                    b.outs.append(b.dma("sp", dbg_y[dc, :, t0:t0 + 128], yd[:], [yd.r()], []))
                b.act(yt[:], pp[:, 0:128], AF.Identity, [pp.r(), cm.modv.r()], [yt.r()], scale=cm.mod(2, dc, col))
                b.stt(u[:, dc, :], xq[:, dc, :], ALPHA, yt[:], ALU.mult, ALU.add, [xq.r(), yt.r()], [u.r()])
            cm.ln1_store(u, name, n, col, t0, 128, x1s_r)
    b.release(mk0)
    for (name, n, col) in segs:
        b.memset(cm.h2[name][:, :, 0:1], 0.0, [cm.h2[name].r()])
    cm.emit_ffn(out_d)
    return b.finish()


def fm_vec(v):
    return np.ascontiguousarray(v.reshape(-1, 128).T)


def kmajor(w):
    K, N = w.shape
    return np.ascontiguousarray(w.reshape(K // 128, 128, N).transpose(1, 0, 2))


def seg_fm(X, half, ncols, pad):
    N = X.shape[0]
    Xl = X if half == 0 else X[::-1]
    out = np.zeros((ncols, D), dtype=np.float32)
    n = min(N, ncols - pad)
    out[pad:pad + n] = Xl[:n]
    return np.ascontiguousarray(out.T.reshape(NCH, 128, ncols))


def unfm(o, half):
    n = o.shape[2]
    X = o.reshape(D, n).T
    return X if half == 0 else X[::-1]


def common_inputs(inp, layer, bidx, half):
    cc = np.stack([fm_vec(inp["c"][bidx]), fm_vec(inp["c_ctx"])], axis=-1)
    cw = inp["ffn_conv_w"][layer]
    taps = [cw[0], cw[1], cw[2]] if half == 0 else [cw[2], cw[1], cw[0]]
    convp = np.stack([fm_vec(t) for t in taps] + [fm_vec(inp["ffn_conv_b"][layer])], axis=-1)
    lnp = np.stack([fm_vec(inp[k][layer]) for k in ("ln1_g", "ln1_b", "ln2_g", "ln2_b")], axis=1)
    return {
        "cc": np.ascontiguousarray(cc),
        "ada_w": kmajor(inp["ada_w"][layer]),
        "ada_b": fm_vec(inp["ada_b"][layer]),
        "lnp": np.ascontiguousarray(lnp),
        "w_up": kmajor(inp["ffn_w_up"][layer]),
        "w_dn": kmajor(inp["ffn_w_down"][layer]),
        "convp": np.ascontiguousarray(convp),
    }


_NC_CACHE = {}


def get_nc(key, fn):
    if key not in _NC_CACHE:
        _NC_CACHE[key] = fn()
    return _NC_CACHE[key]


def run_layer0(inp, x_lat, x_ctx, stage=9):
    nc = build_layer0(stage)
    maps = []
    for core in range(8):
        bidx, half = core // 2, core % 2
        m = common_inputs(inp, 0, bidx, half)
        m["xin_l"] = seg_fm(x_lat[bidx], half, 8 + HALF + 1 + 15, 8)
        m["xin_c"] = seg_fm(x_ctx[bidx], half, 8 + CHALF + 1 + 15, 8)
        m["pool_w"] = np.ascontiguousarray(inp["pool_w"][0].reshape(4, 2, 128, 256).transpose(2, 0, 1, 3))
        m["pool_p"] = np.ascontiguousarray(np.stack([fm_vec(inp["pool_b"][0]), fm_vec(inp["pool_scale"][0])], axis=1))
        fl = np.zeros((128, 2), np.float32)
        fl[:, half] = 1.0
        m["flags"] = fl
        corr = np.zeros((128, NCH, 8), np.float32)
        for c in range(NCH):
            w = POOL_WINDOWS[c // 2]
            for tau in range(8):
                cnt = min(w, tau + w // 2 + (1 if half == 1 else 0))
                corr[:, c, tau] = np.float32(w) / np.float32(cnt)
        m["corr"] = corr
        maps.append(m)
    res = run_bass_kernel_spmd(nc, maps, core_ids=list(range(8)))
    if stage < 9:
        return res
    xl = np.zeros((4, SEQ, D), np.float32)
    xc = np.zeros((4, CTX, D), np.float32)
    for core in range(8):
        bidx, half = core // 2, core % 2
        r = res.results[core]
        ol = unfm(np.asarray(r["out_l"]), half)
        oc = unfm(np.asarray(r["out_c"]), half)
        if half == 0:
            xl[bidx, :HALF] = ol
            xc[bidx, :CHALF] = oc
        else:
            xl[bidx, HALF:] = ol
            xc[bidx, CHALF:] = oc
    return xl, xc


def collect_lat(res, key="out_l"):
    xl = np.zeros((4, SEQ, D), np.float32)
    for core in range(8):
        bidx, half = core // 2, core % 2
        ol = unfm(np.asarray(res.results[core][key]), half)
        if half == 0:
            xl[bidx, :HALF] = ol
        else:
            xl[bidx, HALF:] = ol
    return xl


def bc128(v):
    return np.ascontiguousarray(np.broadcast_to(v[None, :], (128,) + v.shape)).astype(np.float32)


def run_layer3(inp, x_lat):
    nc = build_layer3()
    maps = []
    for core in range(8):
        bidx, half = core // 2, core % 2
        m = common_inputs(inp, 3, bidx, half)
        m["xin_l"] = seg_fm(x_lat[bidx], half, GM_NT, 0)
        m["g_w_in"] = kmajor(inp["gmlp_w_in"][0])
        bi = inp["gmlp_b_in"][0]
        m["g_b_u"] = fm_vec(bi[:GH])
        m["g_b_v"] = bc128(bi[GH:])
        m["g_ln_g"] = bc128(inp["gmlp_ln_g"][0])
        m["g_ln_b"] = bc128(inp["gmlp_ln_b"][0])
        ws = inp["gmlp_w_s"][0]
        bs = inp["gmlp_b_s"][0]
        if half == 1:
            ws = ws[:, ::-1, ::-1]
            bs = bs[:, ::-1]
        m["g_wsT"] = np.ascontiguousarray(ws.transpose(2, 0, 1))
        m["g_bs"] = bc128(np.ascontiguousarray(bs))
        m["g_w_out"] = kmajor(inp["gmlp_w_out"][0])
        maps.append(m)
    res = run_bass_kernel_spmd(nc, maps, core_ids=list(range(8)))
    return collect_lat(res)


def ssm_pair_layout(a):
    return np.ascontiguousarray(a.reshape(32, 128).T)


def run_ssm_scan(inp, x_lat, x_ctx):
    nc = build_ssm_scan()
    maps = []
    jj = bc128(np.arange(128, dtype=np.float32))
    for core in range(8):
        bidx, dr = core // 2, core % 2
        seq = np.concatenate([x_ctx[bidx], x_lat[bidx]], axis=0) if dr == 0 else \
            np.concatenate([x_ctx[bidx][::-1], x_lat[bidx][::-1]], axis=0)
        m = {"xs": np.ascontiguousarray(seq.T.reshape(NCH, 128, SS_NT))}
        cc = np.stack([fm_vec(inp["c"][bidx]), fm_vec(inp["c_ctx"])], axis=-1)
        m["cc"] = np.ascontiguousarray(cc)
        m["ada_w"] = kmajor(inp["ada_w"][2])
        m["ada_b"] = fm_vec(inp["ada_b"][2])
        m["lnp"] = np.ascontiguousarray(np.stack([fm_vec(inp[k][2]) for k in ("ln1_g", "ln1_b", "ln2_g", "ln2_b")], axis=1))
        ldt = np.repeat(inp["ssm_log_dt"][0, dr].reshape(32, 2), 64, axis=1)
        m["s_lam"] = np.ascontiguousarray(np.stack([ssm_pair_layout(inp["ssm_lambda_re"][0, dr]),
                                                    ssm_pair_layout(inp["ssm_lambda_im"][0, dr]),
                                                    np.ascontiguousarray(ldt.T)], axis=1))
        bt = np.zeros((128, 2, 32, 128), np.float32)
        ctp = np.zeros((128, 2, 32, 32), np.float32)
        for ri, (bk, ck_) in enumerate((("ssm_b_re", "ssm_c_re"), ("ssm_b_im", "ssm_c_im"))):
            Bm = inp[bk][0, dr]
            Cm = inp[ck_][0, dr]
            for q in range(32):
                for g2 in range(2):
                    g = 2 * q + g2
                    r0 = (q % 4) * 32 + g2 * 16
                    bt[r0:r0 + 16, ri, q, g2 * 64:(g2 + 1) * 64] = Bm[g].T
                    ctp[g2 * 64:(g2 + 1) * 64, ri, q, g2 * 16:(g2 + 1) * 16] = Cm[g].T
        m["s_bt"] = bt
        m["s_ct"] = ctp
        m["s_jj"] = jj
        maps.append(m)
    res = run_bass_kernel_spmd(nc, maps, core_ids=list(range(8)))
    yf = np.zeros((4, SEQ, D), np.float32)
    yr = np.zeros((4, SEQ, D), np.float32)
    for core in range(8):
        bidx, dr = core // 2, core % 2
        o = np.asarray(res.results[core]["yo"]).reshape(D, SEQ).T
        if dr == 0:
            yf[bidx] = o
        else:
            yr[bidx] = o[::-1]
    return yf, yr


def run_layer2b(inp, x_lat, yf, yr):
    nc = build_layer2b()
    maps = []
    NT = HALF + 1
    for core in range(8):
        bidx, half = core // 2, core % 2
        m = common_inputs(inp, 2, bidx, half)
        m["xin_l"] = seg_fm(x_lat[bidx], half, NT, 0)
        m["yf"] = seg_fm(yf[bidx], half, NT, 0)
        m["yr"] = seg_fm(yr[bidx], half, NT, 0)
        m["s_d"] = fm_vec(inp["ssm_d"][0])
        m["s_wa"] = kmajor(inp["ssm_w_glu_a"][0])
        m["s_wb"] = kmajor(inp["ssm_w_glu_b"][0])
        maps.append(m)
    res = run_bass_kernel_spmd(nc, maps, core_ids=list(range(8)))
    return collect_lat(res)


def rope_tables(half):
    tau = np.arange(AT_NK)
    t = tau if half == 0 else (SEQ - 1 - tau)
    row = (t // 64).astype(np.float32)
    colp = (t % 64).astype(np.float32)
    p = np.arange(128)
    d = p % 64
    freqs = (np.float32(10000.0) ** (-(np.arange(16, dtype=np.float32)) / np.float32(16))).astype(np.float32)
    f = freqs[d % 16]
    pos = np.where((d // 32)[:, None] == 0, row[None, :], colp[None, :]).astype(np.float32)
    ang = (pos * f[:, None]).astype(np.float32)
    return np.cos(ang).astype(np.float32), np.sin(ang).astype(np.float32)


def swap16(w):
    K, N = w.shape
    return np.ascontiguousarray(w.reshape(K, N // 32, 2, 16)[:, :, ::-1, :].reshape(K, N))


def run_layer1(inp, x_lat, x_ctx):
    nc = build_layer1()
    maps = []
    wqkv = inp["attn_w_qkv"][0]
    wq, wk, wv = wqkv[:, :1024], wqkv[:, 1024:1280], wqkv[:, 1280:1536]
    z64 = np.zeros((1024, 64), np.float32)
    wk_s = swap16(wk)
    kd = np.concatenate([np.concatenate([wk[:, h * 64:(h + 1) * 64], z64, z64, wk[:, h * 64:(h + 1) * 64]], axis=1)
                         for h in range(4)], axis=1)
    kdr = np.concatenate([np.concatenate([wk_s[:, h * 64:(h + 1) * 64], z64, z64, wk_s[:, h * 64:(h + 1) * 64]], axis=1)
                          for h in range(4)], axis=1)
    wall = np.concatenate([wq, kd], axis=1)
    wallr = np.concatenate([swap16(wq), kdr], axis=1)
    a_wq = np.ascontiguousarray(kmajor(wall).reshape(128, NCH, 16, 128))
    a_wqr = np.ascontiguousarray(kmajor(wallr).reshape(128, NCH, 16, 128))
    a_wv = kmajor(wv)
    a_wo = np.ascontiguousarray(inp["attn_w_o"][0].reshape(16, 64, D).transpose(1, 0, 2))
    jj, ii = np.meshgrid(np.arange(128), np.arange(128), indexing="ij")
    mL = np.where(jj >= ii, 0.0, NEG).astype(np.float32)
    mU = np.where(jj <= ii, 0.0, NEG).astype(np.float32)
    mask = np.ascontiguousarray(np.stack([np.tile(mL, (1, 4)), np.tile(mU, (1, 4))], axis=1))
    sink = inp["attn_sink"][0]
    a_sink = np.ascontiguousarray(np.broadcast_to(sink[None, :, None], (64, 16, 128))).astype(np.float32)
    for core in range(8):
        bidx, half = core // 2, core % 2
        m = common_inputs(inp, 1, bidx, half)
        m["xin_l"] = seg_fm(x_lat[bidx], half, AT_NK, 0)
        m["xin_c"] = seg_fm(x_ctx[bidx], half, CTX, 0)
        m["a_wq"], m["a_wqr"], m["a_wv"], m["a_wo"] = a_wq, a_wqr, a_wv, a_wo
        cs, sn = rope_tables(half)
        m["a_cos"], m["a_sin"] = cs, sn
        m["a_mask"] = mask
        m["a_sink"] = a_sink
        m["a_ident"] = np.eye(128, dtype=np.float32)
        maps.append(m)
    res = run_bass_kernel_spmd(nc, maps, core_ids=list(range(8)))
    import os
    if os.environ.get("DBG_A"):
        return res, None
    xl = collect_lat(res)
    xc = np.zeros((4, CTX, D), np.float32)
    for core in range(8):
        bidx, half = core // 2, core % 2
        oc = unfm(np.asarray(res.results[core]["out_c"]), half)
        if half == 0:
            xc[bidx, :CHALF] = oc
        else:
            xc[bidx, CHALF:] = oc
    return xl, xc


def kernel(**inputs):
    inp = {k: np.asarray(v) for k, v in inputs.items()}
    x_lat = inp["x"].astype(np.float32, copy=False)
    x_ctx = inp["ctx"].astype(np.float32, copy=False)
    x_lat, x_ctx = run_layer0(inp, x_lat, x_ctx)
    x_lat, x_ctx = run_layer1(inp, x_lat, x_ctx)
    yf, yr = run_ssm_scan(inp, x_lat, x_ctx)
    x_lat = run_layer2b(inp, x_lat, yf, yr)
    x_lat = run_layer3(inp, x_lat)
    return np.ascontiguousarray(x_lat.astype(np.float32))
```
